# Optimizing a Trainium2 kernel written in Bass

```python
import math
import jax, jax.numpy as jnp
from jax import lax
import numpy as np

D_MODEL = 1024
BATCH = 1
SEQ = 16384
DEPTH = 2
DEC_BATCH = 8
DEC_SEQ = 32
PAST_LEN = 2048

CHUNK = 64
QBLOCK = 128
N_EVEN = (DEPTH + 1) // 2
N_ODD = DEPTH // 2
D_FF = 4 * D_MODEL
EPS = 1e-6
NEG = -1e30
F32 = jnp.float32

MLA_HEADS = 8
MLA_Q_RANK = 256
MLA_KV_RANK = 128
MLA_NOPE = 64
MLA_ROPE = 32
MLA_V = 64
ROPE_BASE = 10000.0
DIFF_HEADS = 4
DIFF_QK = 64
DIFF_V = 2 * DIFF_QK
T5_BUCKETS = 32
T5_MAX_DIST = 128
SB_HEADS = 8
SB_DIM = 64
CA_HEADS = 8
CA_DIM = 64
CA_LEFT_CHUNKS = 8
CA_BAND = CA_LEFT_CHUNKS * CHUNK
CA_MAX_REL = 128

EVEN_SIZES = [MLA_Q_RANK, MLA_KV_RANK, MLA_ROPE, DIFF_HEADS * 2 * DIFF_QK, DIFF_HEADS * 2 * DIFF_QK, DIFF_HEADS * DIFF_V]
EVEN_IN = sum(EVEN_SIZES)
EVEN_MIX = MLA_HEADS * MLA_V + DIFF_HEADS * DIFF_V
ODD_SIZES = [SB_HEADS * SB_DIM] * 3 + [CA_HEADS * CA_DIM] * 3
ODD_IN = sum(ODD_SIZES)
ODD_MIX = SB_HEADS * SB_DIM + CA_HEADS * CA_DIM

kernel_name = 'hybrid_chunk_stream_encoder_step'


def split_cols(a, sizes):
    out, start = [], 0
    for s in sizes:
        out.append(a[..., start:start + s])
        start += s
    return out


def rmsnorm(x, g):
    xf = x.astype(F32)
    y = xf * lax.rsqrt(jnp.mean(xf * xf, axis=-1, keepdims=True) + EPS)
    return (y * g.astype(F32)).astype(x.dtype)


def rope(x, pos):
    half = x.shape[-1] // 2
    inv = ROPE_BASE ** (-jnp.arange(half, dtype=F32) / half)
    ang = pos.astype(F32)[:, None] * inv[None, :]
    shp = (pos.shape[0],) + (1,) * (x.ndim - 3) + (half,)
    cos = jnp.cos(ang).reshape(shp)
    sin = jnp.sin(ang).reshape(shp)
    xf = x.astype(F32)
    x1, x2 = xf[..., :half], xf[..., half:]
    return jnp.concatenate([x1 * cos - x2 * sin, x2 * cos + x1 * sin], axis=-1).astype(x.dtype)


def t5_bucket(rel):
    nb = T5_BUCKETS // 2
    max_exact = nb // 2
    ret = jnp.where(rel > 0, nb, 0)
    n = jnp.abs(rel)
    nf = jnp.maximum(n, 1).astype(F32)
    large = max_exact + (jnp.log(nf / max_exact) / math.log(T5_MAX_DIST / max_exact) * (nb - max_exact)).astype(jnp.int32)
    large = jnp.minimum(large, nb - 1)
    return ret + jnp.where(n < max_exact, n, large)


def chunk_causal(q_pos, k_pos):
    return (k_pos[None, :] // CHUNK) <= (q_pos[:, None] // CHUNK)


def sweep_queries(fn, q_args, q_pos):
    sq = q_pos.shape[0]
    if sq <= QBLOCK or sq % QBLOCK:
        return fn(q_args, q_pos)
    nb = sq // QBLOCK
    blocks = tuple(jnp.moveaxis(a.reshape((a.shape[0], nb, QBLOCK) + a.shape[2:]), 1, 0) for a in q_args)
    out = lax.map(lambda bp: fn(bp[0], bp[1]), (blocks, q_pos.reshape(nb, QBLOCK)))
    out = jnp.moveaxis(out, 0, 1)
    return out.reshape((out.shape[0], sq) + out.shape[3:])


def mla_attend(c_q, q_pos, ckv, krope, k_pos, w_uq, w_ukv):
    b, s = c_q.shape[:2]
    q = (c_q @ w_uq).reshape(b, s, MLA_HEADS, MLA_NOPE + MLA_ROPE)
    q_nope, q_rope = q[..., :MLA_NOPE], rope(q[..., MLA_NOPE:], q_pos)
    kv = (ckv @ w_ukv).reshape(b, ckv.shape[1], MLA_HEADS, MLA_NOPE + MLA_V)
    k_nope, v = kv[..., :MLA_NOPE], kv[..., MLA_NOPE:]
    scale = (MLA_NOPE + MLA_ROPE) ** -0.5

    def block(qs, qp):
        qn, qr = qs
        sc = (jnp.einsum('bqhd,bkhd->bhqk', qn, k_nope, preferred_element_type=F32)
              + jnp.einsum('bqhr,bkr->bhqk', qr, krope, preferred_element_type=F32)) * scale
        sc = jnp.where(chunk_causal(qp, k_pos), sc, NEG)
        p = jax.nn.softmax(sc, axis=-1).astype(v.dtype)
        return jnp.einsum('bhqk,bkhd->bqhd', p, v)

    o = sweep_queries(block, (q_nope, q_rope), q_pos)
    return o.reshape(b, s, MLA_HEADS * MLA_V)


def diff_lambda(lam_vecs, lam_init):
    lv = lam_vecs.astype(F32)
    return jnp.exp(jnp.sum(lv[0] * lv[1])) - jnp.exp(jnp.sum(lv[2] * lv[3])) + lam_init


def diff_attend(q, q_pos, k, v, k_pos, lam, lam_init, subln, t5_table):
    b, s = q.shape[:2]
    scale = DIFF_QK ** -0.5

    def block(qs, qp):
        (qb,) = qs
        sc = jnp.einsum('bqhmd,bkhmd->bmhqk', qb, k, preferred_element_type=F32) * scale
        bias = jnp.moveaxis(t5_table[t5_bucket(k_pos[None, :] - qp[:, None])], -1, 0).astype(F32)
        sc = jnp.where(chunk_causal(qp, k_pos), sc + bias, NEG)
        p = jax.nn.softmax(sc, axis=-1)
        a = (p[:, 0] - lam * p[:, 1]).astype(v.dtype)
        return jnp.einsum('bhqk,bkhd->bqhd', a, v)

    o = sweep_queries(block, (q,), q_pos)
    o = rmsnorm(o, subln) * (1.0 - lam_init)
    return o.reshape(b, s, DIFF_HEADS * DIFF_V)


def stick_breaking(q, q_pos, k, v, k_pos):
    b, s = q.shape[:2]
    scale = SB_DIM ** -0.5

    def block(qs, qp):
        (qb,) = qs
        z = jnp.einsum('bqhd,bkhd->bhqk', qb, k, preferred_element_type=F32) * scale
        mask = k_pos[None, :] < qp[:, None]
        log_beta = jax.nn.log_sigmoid(z)
        log_1m = jnp.where(mask, jax.nn.log_sigmoid(-z), 0.0)
        nxt = jnp.concatenate([log_1m[..., 1:], jnp.zeros_like(log_1m[..., :1])], axis=-1)
        after = lax.cumsum(nxt, axis=3, reverse=True)
        a = jnp.where(mask, jnp.exp(log_beta + after), 0.0).astype(v.dtype)
        return jnp.einsum('bhqk,bkhd->bqhd', a, v)

    o = sweep_queries(block, (q,), q_pos)
    return o.reshape(b, s, SB_HEADS * SB_DIM)


def ca_bias(table, rel):
    idx = jnp.clip(rel, -CA_MAX_REL, CA_MAX_REL) + CA_MAX_REL
    return jnp.moveaxis(table[idx], -1, 0).astype(F32)


def chunk_band_prompt(q, k, v, table):
    b, s, h, dh = q.shape
    nc = s // CHUNK
    nb = CA_LEFT_CHUNKS + 1
    pad = ((0, 0), (CA_BAND, 0), (0, 0), (0, 0))
    kc = jnp.pad(k, pad).reshape(b, nc + CA_LEFT_CHUNKS, CHUNK, h, dh)
    vc = jnp.pad(v, pad).reshape(b, nc + CA_LEFT_CHUNKS, CHUNK, h, dh)
    idx = jnp.arange(nc)[:, None] + jnp.arange(nb)[None, :]
    k_band = kc[:, idx].reshape(b, nc, nb * CHUNK, h, dh)
    v_band = vc[:, idx].reshape(b, nc, nb * CHUNK, h, dh)
    qc = q.reshape(b, nc, CHUNK, h, dh)
    sc = jnp.einsum('bcqhd,bckhd->bchqk', qc, k_band, preferred_element_type=F32) * (dh ** -0.5)
    qi = jnp.arange(CHUNK)
    kj = jnp.arange(nb * CHUNK)
    sc = sc + ca_bias(table, kj[None, :] - CA_BAND - qi[:, None])
    valid = (jnp.arange(nc)[:, None] - CA_LEFT_CHUNKS) * CHUNK + kj[None, :] >= 0
    sc = jnp.where(valid[None, :, None, None, :], sc, NEG)
    p = jax.nn.softmax(sc, axis=-1).astype(v.dtype)
    o = jnp.einsum('bchqk,bckhd->bcqhd', p, v_band)
    return o.reshape(b, s, h * dh)


def chunk_band_direct(q, q_pos, k, v, k_pos, table):
    b, s, h, dh = q.shape
    sc = jnp.einsum('bqhd,bkhd->bhqk', q, k, preferred_element_type=F32) * (dh ** -0.5)
    sc = sc + ca_bias(table, k_pos[None, :] - q_pos[:, None])
    qc, kc = q_pos[:, None] // CHUNK, k_pos[None, :] // CHUNK
    mask = (kc <= qc) & (kc >= qc - CA_LEFT_CHUNKS)
    p = jax.nn.softmax(jnp.where(mask, sc, NEG), axis=-1).astype(v.dtype)
    return jnp.einsum('bhqk,bkhd->bqhd', p, v).reshape(b, s, h * dh)


def even_mixer(h, pos, past, past_pos, w_in, q_norm, kv_norm, w_uq, w_ukv, lam_vecs, subln, t5_table, w_out, lam_init):
    b, s, _ = h.shape
    c_q, c_kv, k_r, dq, dk, dv = split_cols(h @ w_in, EVEN_SIZES)
    c_q = rmsnorm(c_q, q_norm)
    c_kv = rmsnorm(c_kv, kv_norm)
    k_r = rope(k_r, pos)
    dk = dk.reshape(b, s, DIFF_HEADS, 2 * DIFF_QK)
    dv = dv.reshape(b, s, DIFF_HEADS, DIFF_V)
    new = (c_kv, k_r, dk, dv)
    if past is None:
        ckv_all, kr_all, dk_all, dv_all = new
        k_pos = pos
    else:
        ckv_all, kr_all, dk_all, dv_all = (jnp.concatenate([pa, nw], axis=1) for pa, nw in zip(past, new))
        k_pos = jnp.concatenate([past_pos, pos])
    o_mla = mla_attend(c_q, pos, ckv_all, kr_all, k_pos, w_uq, w_ukv)
    lam = diff_lambda(lam_vecs, lam_init)
    o_diff = diff_attend(dq.reshape(b, s, DIFF_HEADS, 2, DIFF_QK), pos,
                         dk_all.reshape(b, dk_all.shape[1], DIFF_HEADS, 2, DIFF_QK), dv_all, k_pos,
                         lam, lam_init, subln, t5_table)
    out = jnp.concatenate([o_mla, o_diff], axis=-1) @ w_out
    return out, new


def odd_mixer(h, pos, past, past_pos, band_pos, w_in, ca_table, w_out):
    b, s, _ = h.shape
    sq, sk, sv, cq, ck, cv = split_cols(h @ w_in, ODD_SIZES)
    sq, sk, sv = (a.reshape(b, s, SB_HEADS, SB_DIM) for a in (sq, sk, sv))
    cq, ck, cv = (a.reshape(b, s, CA_HEADS, CA_DIM) for a in (cq, ck, cv))
    if past is None:
        o_sb = stick_breaking(sq, pos, sk, sv, pos)
        o_ca = chunk_band_prompt(cq, ck, cv, ca_table)
        nb = min(CA_BAND, s)
        new = (sk, sv, ck[:, s - nb:], cv[:, s - nb:])
    else:
        sbk_p, sbv_p, cak_p, cav_p = past
        o_sb = stick_breaking(sq, pos, jnp.concatenate([sbk_p, sk], axis=1), jnp.concatenate([sbv_p, sv], axis=1),
                              jnp.concatenate([past_pos, pos]))
        k_all = jnp.concatenate([cak_p, ck], axis=1)
        v_all = jnp.concatenate([cav_p, cv], axis=1)
        o_ca = chunk_band_direct(cq, pos, k_all, v_all, jnp.concatenate([band_pos, pos]), ca_table)
        nb = cak_p.shape[1]
        new = (sk, sv, k_all[:, k_all.shape[1] - nb:], v_all[:, v_all.shape[1] - nb:])
    out = jnp.concatenate([o_sb, o_ca], axis=-1) @ w_out
    return out, new


def trunk(x, pos, caches, past_pos, band_pos, weights):
    (norm_mix, norm_ff, w_in_even, mla_q_norm, mla_kv_norm, mla_w_uq, mla_w_ukv, diff_lambda_vecs,
     diff_subln, t5_bias, w_out_even, w_in_odd, ca_rel_bias, w_out_odd, w_ff1, w_ff2, final_norm) = weights
    new_even, new_odd = [], []
    for l in range(DEPTH):
        i = l // 2
        h = rmsnorm(x, norm_mix[l])
        if l % 2 == 0:
            past = None if caches is None else tuple(c[i] for c in caches[:4])
            lam_init = 0.8 - 0.6 * math.exp(-0.3 * l)
            out, new = even_mixer(h, pos, past, past_pos, w_in_even[i], mla_q_norm[i], mla_kv_norm[i],
                                  mla_w_uq[i], mla_w_ukv[i], diff_lambda_vecs[i], diff_subln[i], t5_bias,
                                  w_out_even[i], lam_init)
            new_even.append(new)
        else:
            past = None if caches is None else tuple(c[i] for c in caches[4:])
            out, new = odd_mixer(h, pos, past, past_pos, band_pos, w_in_odd[i], ca_rel_bias[i], w_out_odd[i])
            new_odd.append(new)
        x = x + out
        h = rmsnorm(x, norm_ff[l])
        x = x + jnp.square(jax.nn.relu(h @ w_ff1[l])) @ w_ff2[l]
    y = rmsnorm(x, final_norm)
    stacked_even = [jnp.stack(rows) for rows in zip(*new_even)]
    stacked_odd = [jnp.stack(rows) for rows in zip(*new_odd)]
    return y, stacked_even + stacked_odd


def setup_inputs(seed: int = 0) -> dict:
    key = jax.random.key(seed)
    ks = iter(jax.random.split(key, 40))

    def nrm(shape, scale):
        return jax.random.normal(next(ks), shape, F32) * scale

    def gain(shape):
        return 1.0 + nrm(shape, 0.02)

    band = min(CA_BAND, PAST_LEN)
    return {
        'x_prompt': nrm((BATCH, SEQ, D_MODEL), 1.0),
        'x_sample': nrm((DEC_BATCH, DEC_SEQ, D_MODEL), 1.0),
        'cache_mla_ckv': nrm((N_EVEN, DEC_BATCH, PAST_LEN, MLA_KV_RANK), 1.0),
        'cache_mla_krope': nrm((N_EVEN, DEC_BATCH, PAST_LEN, MLA_ROPE), 1.0),
        'cache_diff_k': nrm((N_EVEN, DEC_BATCH, PAST_LEN, DIFF_HEADS, 2 * DIFF_QK), 1.0),
        'cache_diff_v': nrm((N_EVEN, DEC_BATCH, PAST_LEN, DIFF_HEADS, DIFF_V), 1.0),
        'cache_sb_k': nrm((N_ODD, DEC_BATCH, PAST_LEN, SB_HEADS, SB_DIM), 1.0),
        'cache_sb_v': nrm((N_ODD, DEC_BATCH, PAST_LEN, SB_HEADS, SB_DIM), 1.0),
        'cache_ca_k': nrm((N_ODD, DEC_BATCH, band, CA_HEADS, CA_DIM), 1.0),
        'cache_ca_v': nrm((N_ODD, DEC_BATCH, band, CA_HEADS, CA_DIM), 1.0),
        'norm_mix': gain((DEPTH, D_MODEL)),
        'norm_ff': gain((DEPTH, D_MODEL)),
        'w_in_even': nrm((N_EVEN, D_MODEL, EVEN_IN), D_MODEL ** -0.5),
        'mla_q_norm': gain((N_EVEN, MLA_Q_RANK)),
        'mla_kv_norm': gain((N_EVEN, MLA_KV_RANK)),
        'mla_w_uq': nrm((N_EVEN, MLA_Q_RANK, MLA_HEADS * (MLA_NOPE + MLA_ROPE)), MLA_Q_RANK ** -0.5),
        'mla_w_ukv': nrm((N_EVEN, MLA_KV_RANK, MLA_HEADS * (MLA_NOPE + MLA_V)), MLA_KV_RANK ** -0.5),
        'diff_lambda_vecs': nrm((N_EVEN, 4, DIFF_QK), 0.1),
        'diff_subln': gain((N_EVEN, DIFF_V)),
        't5_bias': nrm((T5_BUCKETS, DIFF_HEADS), 0.5),
        'w_out_even': nrm((N_EVEN, EVEN_MIX, D_MODEL), EVEN_MIX ** -0.5),
        'w_in_odd': nrm((N_ODD, D_MODEL, ODD_IN), D_MODEL ** -0.5),
        'ca_rel_bias': nrm((N_ODD, 2 * CA_MAX_REL + 1, CA_HEADS), 0.5),
        'w_out_odd': nrm((N_ODD, ODD_MIX, D_MODEL), ODD_MIX ** -0.5),
        'w_ff1': nrm((DEPTH, D_MODEL, D_FF), D_MODEL ** -0.5),
        'w_ff2': nrm((DEPTH, D_FF, D_MODEL), D_FF ** -0.5),
        'final_norm': gain((D_MODEL,)),
    }


def reference(x_prompt, x_sample, cache_mla_ckv, cache_mla_krope, cache_diff_k, cache_diff_v,
              cache_sb_k, cache_sb_v, cache_ca_k, cache_ca_v,
              norm_mix, norm_ff, w_in_even, mla_q_norm, mla_kv_norm, mla_w_uq, mla_w_ukv,
              diff_lambda_vecs, diff_subln, t5_bias, w_out_even, w_in_odd, ca_rel_bias, w_out_odd,
              w_ff1, w_ff2, final_norm):
    weights = (norm_mix, norm_ff, w_in_even, mla_q_norm, mla_kv_norm, mla_w_uq, mla_w_ukv, diff_lambda_vecs,
               diff_subln, t5_bias, w_out_even, w_in_odd, ca_rel_bias, w_out_odd, w_ff1, w_ff2, final_norm)
    seq = x_prompt.shape[1]
    dec_seq = x_sample.shape[1]
    past_len = cache_mla_ckv.shape[2]
    band_len = cache_ca_k.shape[2]
    pos_p = jnp.arange(seq, dtype=jnp.int32)
    pos_s = past_len + jnp.arange(dec_seq, dtype=jnp.int32)
    past_pos = jnp.arange(past_len, dtype=jnp.int32)
    band_pos = jnp.arange(past_len - band_len, past_len, dtype=jnp.int32)

    y_prompt, new_p = trunk(x_prompt, pos_p, None, None, None, weights)
    caches = (cache_mla_ckv, cache_mla_krope, cache_diff_k, cache_diff_v,
              cache_sb_k, cache_sb_v, cache_ca_k, cache_ca_v)
    y_sample, new_s = trunk(x_sample, pos_s, caches, past_pos, band_pos, weights)

    p_ckv, p_krope, p_dk, p_dv, p_sbk, p_sbv, p_cak, p_cav = new_p
    s_ckv, s_krope, s_dk, s_dv, s_sbk, s_sbv, s_cak, s_cav = new_s
    return (y_prompt, y_sample, p_ckv, p_krope, p_dk, p_dv, p_sbk, p_sbv, p_cak, p_cav,
            s_ckv, s_krope, s_dk, s_dv, s_sbk, s_sbv, s_cak, s_cav)
```

```python
import math
import os
import numpy as np
from contextlib import ExitStack
import concourse.bass as bass
import concourse.mybir as mybir
from concourse.bass_utils import run_bass_kernel_spmd

F32 = mybir.dt.float32
BF16 = mybir.dt.bfloat16
AF = mybir.ActivationFunctionType
ALU = mybir.AluOpType

NCORES = 8
D = 1024
SEQ = 16384
NT = SEQ // 128
DSEQ = 32
PAST = 2048
EPS = 1e-6
TOK = SEQ // NCORES
TB = TOK + DSEQ


class Res:
    __slots__ = ("w", "r")

    def __init__(self):
        self.w = None
        self.r = []


class Prog:
    ENG = ("pe", "act", "dve", "pool", "sp")

    def __init__(self, nc, es):
        self.nc = nc
        self.es = es
        self.ops = {e: [] for e in self.ENG}
        self.sems = {}
        self.val = {}
        self.waited = {e: {} for e in self.ENG}
        self.res = {}
        for e in ("pe", "act", "dve", "pool"):
            self._sem("E_" + e)

    def _sem(self, name):
        if name not in self.sems:
            self.sems[name] = self.es.enter_context(self.nc.semaphore(name))
            self.val[name] = 0
        return self.sems[name]

    def _r(self, key):
        r = self.res.get(key)
        if r is None:
            r = self.res[key] = Res()
        return r

    def _deps(self, eng, reads, writes):
        need = {}

        def add(d):
            if d is not None and need.get(d[0], 0) < d[1]:
                need[d[0]] = d[1]

        for k in reads:
            add(self._r(k).w)
        for k in writes:
            r = self._r(k)
            add(r.w)
            for d in r.r:
                add(d)
        out = []
        wd = self.waited[eng]
        own = "E_" + eng
        for s, v in need.items():
            if s == own and eng == "pe":
                continue
            if wd.get(s, 0) >= v:
                continue
            wd[s] = v
            out.append((s, v))
        return out

    def _record(self, sname, v, reads, writes):
        for k in reads:
            r = self._r(k)
            r.r = [d for d in r.r if d[0] != sname]
            r.r.append((sname, v))
        for k in writes:
            r = self._r(k)
            r.w = (sname, v)
            r.r = []

    def op(self, eng, fn, reads=(), writes=()):
        waits = self._deps(eng, reads, writes)
        sname = "E_" + eng
        self.val[sname] += 1
        self._record(sname, self.val[sname], reads, writes)
        self.ops[eng].append((waits, fn, sname, 1))

    def dma(self, queue, out, in_, sem, reads=(), writes=(), **kw):
        waits = self._deps(queue, reads, writes)
        sname = "D_" + sem
        self._sem(sname)
        self.val[sname] += 16
        self._record(sname, self.val[sname], reads, writes)
        self.ops[queue].append((waits, lambda e: e.dma_start(out=out, in_=in_, **kw), sname, 16))

    def barrier(self):
        cur = [(s, v) for s, v in self.val.items() if v > 0]
        for e in self.ENG:
            waits = []
            for s, v in cur:
                if self.waited[e].get(s, 0) < v:
                    self.waited[e][s] = v
                    waits.append((s, v))
            if waits:
                self.ops[e].append((waits, None, None, 0))

    def finish(self):
        self.flush()

    def flush(self):
        nc = self.nc
        final = [(s, v) for s, v in self.val.items() if v > 0]
        engmap = {"pe": "tensor", "act": "scalar", "dve": "vector", "pool": "gpsimd", "sp": "sync"}
        with nc.Block() as block:
            for e, attr in engmap.items():
                def body(eng, ops=self.ops[e], last=(e == "sp")):
                    for waits, fn, sname, inc in ops:
                        for s, v in waits:
                            eng.wait_ge(self.sems[s], v)
                        if fn is not None:
                            fn(eng).then_inc(self.sems[sname], inc)
                    if last:
                        for s, v in final:
                            eng.wait_ge(self.sems[s], v)

                getattr(block, attr)(body)
        for e in self.ENG:
            self.ops[e] = []
            for s, v in final:
                self.waited[e][s] = v

    def collective(self, kind, ins, outs):
        nc = self.nc
        self._sem("CC")
        self.val["CC"] += 1
        v = self.val["CC"]
        with nc.Block() as block:
            def body(g):
                g.collective_compute(kind, ALU.bypass, replica_groups=[list(range(NCORES))],
                                     ins=[ins.opt()], outs=[outs.opt()]).then_inc(self.sems["CC"])
                g.wait_ge(self.sems["CC"], v)
            block.gpsimd(body)
        for e in self.ENG:
            self.waited[e]["CC"] = v


PSUM_KEYS = {"b0", "b1", "b2", "b3", "b4", "b5", "t0", "t1"}


def _key(ap):
    return getattr(ap, "tensor", ap).name


class KB:
    def __init__(self, nc, es):
        self.nc = nc
        self.es = es
        self.P = Prog(nc, es)
        self.bank = [es.enter_context(nc.psum_tensor("b%d" % i, [128, 512], F32)) for i in range(6)]
        self.tbank = [es.enter_context(nc.psum_tensor("t%d" % i, [128, 1024], BF16)) for i in range(2)]
        self.cnt = 0
        self.ident = self.sb("ident", [128, 128], BF16)
        self.ones32 = self.sb("ones32", [128, 128], F32)
        self.onesb = self.sb("onesb", [128, 512], BF16)
        tmp = self.sb("ident32", [128, 128], F32)
        self.op("pool", lambda e: e.memset(self.ones32[:], 1.0), w=[self.ones32])
        self.op("pool", lambda e: e.memset(self.onesb[:], 1.0), w=[self.onesb])
        self.op("pool", lambda e: e.affine_select(out=tmp[:], in_=self.ones32[:], pattern=[[1, 128]],
                                                  compare_op=ALU.is_equal, fill=0.0, base=0,
                                                  channel_multiplier=-1), r=[self.ones32], w=[tmp])
        self.copy("dve", self.ident[:], tmp[:])

    def sb(self, name, shape, dt, es=None):
        nb = int(np.prod(shape[1:])) * (4 if dt == F32 else 2)
        self.sbuf_bytes = getattr(self, "sbuf_bytes", 0) + ((nb + 31) // 32) * 32
        assert self.sbuf_bytes <= 186 * 1024, ("SBUF budget exceeded", name, self.sbuf_bytes)
        return (es or self.es).enter_context(self.nc.sbuf_tensor(name, shape, dt))

    def din(self, name, shape, dt=F32):
        return self.nc.dram_tensor(name, list(shape), dt, kind="ExternalInput").ap()

    def dout(self, name, shape, dt=F32):
        return self.nc.dram_tensor(name, list(shape), dt, kind="ExternalOutput").ap()

    def dscr(self, name, shape, dt=F32):
        return self.nc.dram_tensor(name, list(shape), dt, kind="Internal").ap()

    def rr(self):
        self.cnt += 1
        return self.cnt

    def op(self, eng, fn, r=(), w=(), rk=None, wk=None):
        reads = rk if rk is not None else [_key(a) if not isinstance(a, str) else a for a in r]
        writes = wk if wk is not None else [_key(a) if not isinstance(a, str) else a for a in w]
        writes = list(writes) + [k for k in reads if k in PSUM_KEYS and k not in writes]
        self.P.op(eng, fn, reads, writes)

    def dma(self, q, out, in_, sem=None, rk=None, wk=None, **kw):
        if q == "pool" and os.environ.get("NO_POOL_DMA", "1") == "1":
            q = "act"
        reads = rk if rk is not None else [_key(in_)]
        writes = wk if wk is not None else [_key(out)]
        if sem is None:
            sem = _key(out) if out.tensor.__class__.__name__.startswith("SB") else "st_" + _key(out)
        self.P.dma(q, out, in_, sem, reads, writes, **kw)

    def act(self, out, in_, func, scale=1.0, bias=0.0, accum=None, rk=None, wk=None):
        r = [in_] + ([bias] if not isinstance(bias, float) else []) + ([scale] if not isinstance(scale, float) else [])
        w = [out] + ([accum] if accum is not None else [])
        kw = {}
        if accum is not None:
            kw["accum_out"] = accum
        self.op("act", lambda e: e.activation(out=out, in_=in_, func=func, scale=scale, bias=bias, **kw),
                r=r, w=w, rk=rk, wk=wk)

    def copy(self, eng, out, in_, rk=None, wk=None):
        if eng == "act":
            self.op("act", lambda e: e.activation(out=out, in_=in_, func=AF.Identity), r=[in_], w=[out], rk=rk, wk=wk)
        else:
            self.op(eng, lambda e: e.tensor_copy(out=out, in_=in_), r=[in_], w=[out], rk=rk, wk=wk)

    def tt(self, eng, out, in0, in1, op, rk=None, wk=None):
        self.op(eng, lambda e: e.tensor_tensor(out=out, in0=in0, in1=in1, op=op), r=[in0, in1], w=[out], rk=rk, wk=wk)

    def ts(self, eng, out, in0, s1, s2, op0, op1=None, rk=None, wk=None):
        r = [in0] + [s for s in (s1, s2) if s is not None and not isinstance(s, (int, float))]
        if op1 is None:
            self.op(eng, lambda e: e.tensor_scalar(out=out, in0=in0, scalar1=s1, scalar2=None, op0=op0),
                    r=r, w=[out], rk=rk, wk=wk)
        else:
            self.op(eng, lambda e: e.tensor_scalar(out=out, in0=in0, scalar1=s1, scalar2=s2, op0=op0, op1=op1),
                    r=r, w=[out], rk=rk, wk=wk)

    def stt(self, out, in0, scalar, in1, op0, op1, rk=None, wk=None):
        r = [in0, in1] + ([scalar] if not isinstance(scalar, (int, float)) else [])
        self.op("dve", lambda e: e.scalar_tensor_tensor(out=out, in0=in0, scalar=scalar, in1=in1, op0=op0, op1=op1),
                r=r, w=[out], rk=rk, wk=wk)

    def recip(self, out, in_, rk=None, wk=None):
        self.op("dve", lambda e: e.reciprocal(out=out, in_=in_), r=[in_], w=[out], rk=rk, wk=wk)

    def mm(self, out, lhsT, rhs, start=True, stop=True, rk=None, wk=None, skip=False):
        self.op("pe", lambda e: e.matmul(out, lhsT, rhs, start=start, stop=stop, skip_group_check=skip),
                r=[lhsT, rhs], w=[out], rk=rk, wk=wk)

    def tr(self, out, in_, rk=None, wk=None):
        n = in_.shape[0]
        self.op("pe", lambda e: e.transpose(out=out, in_=in_, identity=self.ident[0:n, 0:n]),
                r=[in_, self.ident], w=[out], rk=rk, wk=wk)

    def evac(self, out, in_, rk=None, wk=None):
        self.copy("act" if self.rr() % 2 else "dve", out, in_, rk=rk, wk=wk)

    def bcast_row(self, name, src_row, n, dt=F32):
        t = self.sb(name, [128, n], dt)
        self.dma("sp", t[:], src_row.partition_broadcast(128))
        return t

    def load_w(self, dst, src, kc, ncols, stages):
        step = 512
        for c0 in range(0, ncols, step):
            c1 = min(ncols, c0 + step)
            stage = stages[self.rr() % len(stages)]
            self.dma("sp", stage[:, 0:kc, 0:c1 - c0], src[:, c0:c1].rearrange("(k p) n -> p k n", p=128))
            self.copy("dve" if self.rr() % 3 else "act", dst[:, 0:kc, c0:c1], stage[:, 0:kc, 0:c1 - c0])

    def rms_scale(self, src, nt, n, ssq, tmp):
        self.act(tmp[0:nt, 0:n], src, AF.Square, accum=ssq[0:nt, 0:1])
        self.act(ssq[0:nt, 0:1], ssq[0:nt, 0:1], AF.Sqrt, scale=1.0 / n, bias=EPS)
        self.recip(ssq[0:nt, 0:1], ssq[0:nt, 0:1])

    def transpose_cols(self, dst, src, nt, ncols, tb):
        kc = ncols // 128
        for k in range(kc):
            self.tr(tb[:, k * 128:k * 128 + nt], src[0:nt, k * 128:(k + 1) * 128])
        self.evac(dst[:, 0:kc, 0:nt], tb[:, 0:kc * 128].rearrange("p (k t) -> p k t", k=kc)[:, :, 0:nt])


def build_dense(layer):
    diff = (layer == 0)
    R = 1536 if diff else 1024
    lam_init = 0.8 - 0.6 * math.exp(-0.3 * layer)
    nc = bass.Bass("TRN2", target_bir_lowering=False)
    with ExitStack() as es:
        kb = KB(nc, es)
        x_in = kb.din("x_in", [TB, D])
        mixT = kb.din("mixT", [R, TB])
        w_out = kb.din("w_out", [D, D])
        w_ff1 = kb.din("w_ff1", [D, 4 * D])
        w_ff2 = kb.din("w_ff2", [4 * D, D])
        g_ff = kb.din("g_ff", [D])
        g_next = kb.din("g_next", [D])
        if diff:
            lamv = kb.din("lamv", [256])
            subln = kb.din("subln", [128])
        x_out = kb.dout("x_out", [TB, D])
        hn_out = kb.dout("hn_out", [TB, D])

        stage = [kb.sb("stage%d" % i, [128, 8, 512], F32) for i in range(2)]
        wout_b = kb.sb("wout_b", [128, 8, D], BF16)
        w1g = kb.sb("w1g", [128, 8, 1024], BF16)
        w2g = kb.sb("w2g", [128, 8, 1024], BF16)
        gff_bc = kb.bcast_row("gff_bc", g_ff, D)
        gnx_bc = kb.bcast_row("gnx_bc", g_next, D)
        kb.load_w(wout_b, w_out, 8, D, stage)
        x1 = kb.sb("x1", [128, 4, D], F32)
        hb = kb.sb("hb", [128, D], BF16)
        hT = kb.sb("hT", [128, 8, 512], BF16)
        oTb = kb.sb("oTb", [128, 8, 512], BF16)
        mst = [kb.sb("mst%d" % i, [128, 512], F32) for i in range(3)]
        aT = kb.sb("aT", [128, 8, 512], BF16)
        rl = [kb.sb("rl%d" % i, [128, 512], F32) for i in range(2)]
        ssq = kb.sb("ssq", [128, 1], F32)
        junk = kb.sb("junk", [128, D], F32)
        hno = kb.sb("hno", [128, D], F32)
        if diff:
            lv = kb.bcast_row("lv", lamv, 256)
            sub_col = kb.sb("sub_col", [128, 1], F32)
            kb.dma("sp", sub_col[:], subln.rearrange("(p o) -> p o", o=1))
            lam = kb.sb("lam", [128, 1], F32)
            l2 = kb.sb("l2", [128, 2], F32)
            lj = kb.sb("lj", [128, 64], F32)
            for j in range(2):
                kb.tt("dve", lj[:], lv[:, 128 * j:128 * j + 64], lv[:, 128 * j + 64:128 * j + 128], ALU.mult)
                kb.op("dve", lambda e, j=j: e.reduce_sum(out=l2[:, j:j + 1], in_=lj[:], axis=mybir.AxisListType.X),
                      r=[lj], w=[l2])
            kb.act(l2[:], l2[:], AF.Exp)
            kb.tt("dve", lam[:], l2[:, 1:2], l2[:, 0:1], ALU.subtract)
            kb.ts("dve", lam[:], lam[:], -lam_init, None, ALU.add)
            dtmp = kb.sb("dtmp", [128, 512], F32)
            dsq = kb.sb("dsq", [128, 512], F32)
            drs = kb.sb("drs", [128, 512], F32)

        blocks = [(i * 512, 512) for i in range(4)] + [(TOK, DSEQ)]
        for (t0, ntb) in blocks:
            ntile = max(1, ntb // 128)
            tl = min(ntb, 128)
            if diff:
                for k in range(4):
                    m = mst[k % 3]
                    kb.dma("sp", m[:, 0:ntb], mixT[k * 128:(k + 1) * 128, t0:t0 + ntb])
                    kb.copy("pool", oTb[:, k, 0:ntb], m[:, 0:ntb])
                for hd in range(4):
                    n1, n2 = mst[0], mst[1]
                    kb.dma("sp", n1[:, 0:ntb], mixT[512 + hd * 128:512 + (hd + 1) * 128, t0:t0 + ntb])
                    kb.dma("sp", n2[:, 0:ntb], mixT[1024 + hd * 128:1024 + (hd + 1) * 128, t0:t0 + ntb])
                    kb.stt(dtmp[:, 0:ntb], n2[:, 0:ntb], lam[:, 0:1], n1[:, 0:ntb], ALU.mult, ALU.add)
                    kb.act(dsq[:, 0:ntb], dtmp[:, 0:ntb], AF.Square)
                    kb.mm(kb.bank[4][:, 0:ntb], kb.ones32[:, :], dsq[:, 0:ntb])
                    kb.act(drs[:, 0:ntb], kb.bank[4][:, 0:ntb], AF.Sqrt, scale=1.0 / 128, bias=EPS)
                    kb.recip(drs[:, 0:ntb], drs[:, 0:ntb])
                    kb.tt("dve", dtmp[:, 0:ntb], dtmp[:, 0:ntb], drs[:, 0:ntb], ALU.mult)
                    kb.ts("dve", oTb[:, 4 + hd, 0:ntb], dtmp[:, 0:ntb], sub_col[:, 0:1], 1.0 - lam_init,
                          ALU.mult, ALU.mult)
            else:
                for k in range(8):
                    m = mst[k % 3]
                    kb.dma("sp", m[:, 0:ntb], mixT[k * 128:(k + 1) * 128, t0:t0 + ntb])
                    kb.copy("pool" if k % 2 else "dve", oTb[:, k, 0:ntb], m[:, 0:ntb])
            for ti in range(ntile):
                kb.dma("sp", x1[0:tl, ti, :], x_in[t0 + ti * 128:t0 + ti * 128 + tl, :])
                for half in range(2):
                    bk = kb.bank[half]
                    for k in range(8):
                        kb.mm(bk[0:tl, :], oTb[:, k, ti * 128:ti * 128 + tl], wout_b[:, k, half * 512:(half + 1) * 512],
                              start=(k == 0), stop=(k == 7))
                    kb.tt("dve", x1[0:tl, ti, half * 512:(half + 1) * 512], x1[0:tl, ti, half * 512:(half + 1) * 512],
                          bk[0:tl, :], ALU.add)
                kb.rms_scale(x1[0:tl, ti, :], tl, D, ssq, junk)
                kb.stt(hb[0:tl, :], x1[0:tl, ti, :], ssq[0:tl, 0:1], gff_bc[0:tl, :], ALU.mult, ALU.mult)
                tb = kb.tbank[ti % 2]
                for k in range(8):
                    kb.tr(tb[:, k * 128:k * 128 + tl], hb[0:tl, k * 128:(k + 1) * 128])
                kb.evac(hT[:, :, ti * 128:ti * 128 + tl],
                        tb[:, :].rearrange("p (k t) -> p k t", k=8)[:, :, 0:tl])
            for g in range(4):
                kb.load_w(w1g, w_ff1[:, g * 1024:(g + 1) * 1024], 8, 1024, stage)
                kb.load_w(w2g, w_ff2[g * 1024:(g + 1) * 1024, :], 8, 1024, stage)
                for fc in range(8):
                    bk = kb.bank[2 + fc % 2]
                    for k in range(8):
                        kb.mm(bk[:, 0:ntb], w1g[:, k, fc * 128:(fc + 1) * 128], hT[:, k, 0:ntb],
                              start=(k == 0), stop=(k == 7))
                    r_ = rl[fc % 2]
                    kb.act(r_[:, 0:ntb], bk[:, 0:ntb], AF.Relu)
                    kb.tt("pool" if fc % 2 else "dve", aT[:, fc, 0:ntb], r_[:, 0:ntb], r_[:, 0:ntb], ALU.mult)
                for ti in range(ntile):
                    for half in range(2):
                        bk = kb.bank[half]
                        for fc in range(8):
                            kb.mm(bk[0:tl, :], aT[:, fc, ti * 128:ti * 128 + tl], w2g[:, fc, half * 512:(half + 1) * 512],
                                  start=(fc == 0), stop=(fc == 7))
                        kb.tt("dve", x1[0:tl, ti, half * 512:(half + 1) * 512],
                              x1[0:tl, ti, half * 512:(half + 1) * 512], bk[0:tl, :], ALU.add)
            for ti in range(ntile):
                kb.dma("pool", x_out[t0 + ti * 128:t0 + ti * 128 + tl, :], x1[0:tl, ti, :])
                kb.rms_scale(x1[0:tl, ti, :], tl, D, ssq, junk)
                kb.stt(hno[0:tl, :], x1[0:tl, ti, :], ssq[0:tl, 0:1], gnx_bc[0:tl, :], ALU.mult, ALU.mult)
                kb.dma("pool", hn_out[t0 + ti * 128:t0 + ti * 128 + tl, :], hno[0:tl, :])
        kb.P.finish()
    return nc


class Attn:
    def __init__(self, kb, sb=True):
        self.kb = kb
        self.Pt = [kb.sb("Pt%d" % i, [128, 512], BF16) for i in range(3)]
        self.n = 0
        if sb:
            self.Us = [kb.sb("Us%d" % i, [128, 512], F32) for i in range(3)]
            self.Ls = [kb.sb("Ls%d" % i, [128, 512], BF16) for i in range(3)]
            self.Es = [kb.sb("Es%d" % i, [128, 512], F32) for i in range(3)]
        self.osb = kb.sb("osb", [128, 512], F32)
        self.rs = kb.sb("rs", [128, 512], F32)
        self.onrm = [kb.sb("onrm%d" % i, [128, 512], F32) for i in range(2)]
        self.hk = kb.sb("hk", [128, 512], F32)
        if not sb:
            return
        self.utri = kb.sb("utri", [128, 128], BF16)
        self.lst = kb.sb("lst", [128, 128], BF16)
        kb.op("pool", lambda e: e.affine_select(out=self.utri[:], in_=kb.onesb[:, 0:128], pattern=[[-1, 128]],
                                                compare_op=ALU.is_ge, fill=0.0, base=0, channel_multiplier=1),
              r=[kb.onesb], w=[self.utri])
        kb.op("pool", lambda e: e.affine_select(out=self.lst[:], in_=kb.onesb[:, 0:128], pattern=[[1, 128]],
                                                compare_op=ALU.is_gt, fill=0.0, base=0, channel_multiplier=-1),
              r=[kb.onesb], w=[self.lst])
        self.n = 0

    def build_E(self, dst, g, base, nk, W, mask=None):
        kb = self.kb
        src = bass.AP(tensor=g.tensor, offset=g.offset + base, ap=[[1, nk], [1, W]])
        kb.dma("sp", self.hk[0:nk, 0:W], src, rk=[_key(g)])
        rev = self.hk[0:nk, 0:W][:, ::-1]
        if mask is None:
            kb.copy("dve", dst, rev)
        else:
            kb.tt("dve", dst, rev, mask, ALU.mult)

    def run(self, kind, nq, W, units, KT, QT, V, scale, out_dma, far_bias=None, dv=64,
            kkeys=None, qkeys=None, vkeys=None):
        kb = self.kb
        B = kb.bank
        flat = []
        for qb in range(nq):
            ul = units(qb)
            for ui, (kt, nk, E, far) in enumerate(ul):
                self.n += 1
                flat.append(dict(qb=qb, first=(ui == 0), last=(ui == len(ul) - 1), kt=kt, nk=nk, E=E, far=far,
                                 sl=self.n % 3, n=self.n))
        sbank = [B[0], B[1], B[4] if kind == "sb" else B[5]]

        def stage_a(d):
            sl, nk = d["sl"], d["nk"]
            q_ap = QT(d["qb"])
            qk = qkeys(d["qb"]) if qkeys else [_key(q_ap)]
            k_ap = KT(d["kt"])
            kk = kkeys(d["kt"]) if kkeys else [_key(k_ap)]
            S = sbank[sl][0:nk, 0:W]
            kb.mm(S, k_ap, q_ap, rk=kk + qk, wk=[_key(sbank[sl])])
            if kind == "softmax":
                Pt = self.Pt[sl][0:nk, 0:W]
                bias = far_bias[0:nk, 0:1] if (d["far"] and far_bias is not None) else 0.0
                kb.act(Pt, S, AF.Exp, scale=scale, bias=bias)
                if d["E"] is not None:
                    kb.tt("pool" if d["n"] % 4 < 2 else "dve", Pt, Pt, d["E"], ALU.mult)
            else:
                U = self.Us[sl][0:nk, 0:W]
                L = self.Ls[sl][0:nk, 0:W]
                kb.act(U, S, AF.Exp, scale=scale)
                kb.act(L, U, AF.Ln, scale=1.0, bias=1.0)
                if d["E"] is not None:
                    kb.tt("pool", L, L, d["E"], ALU.mult)

        def sb_head(d):
            sl, nk = d["sl"], d["nk"]
            L = self.Ls[sl][0:nk, 0:W]
            kb.mm(B[5][:, 0:W], self.utri[0:nk, :], L, start=d["first"], stop=True, skip=True)
            kb.act(self.Es[sl][0:nk, 0:W], B[5][0:nk, 0:W], AF.Exp, scale=-1.0)

        def stage_b1(d):
            sl, nk = d["sl"], d["nk"]
            Pt = self.Pt[sl][0:nk, 0:W]
            kb.tt("dve", Pt, self.Us[sl][0:nk, 0:W], self.Es[sl][0:nk, 0:W], ALU.mult)
            if d["E"] is not None:
                kb.tt("pool", Pt, Pt, d["E"], ALU.mult)
            if not d["last"]:
                kb.mm(B[5][:, 0:W], self.lst[0:nk, :], self.Ls[sl][0:nk, 0:W], start=False, stop=True, skip=True)

        def stage_b(d):
            sl, nk, qb = d["sl"], d["nk"], d["qb"]
            Pt = self.Pt[sl][0:nk, 0:W]
            v_aps = V(d["kt"])
            nV = len(v_aps)
            vk = vkeys(d["kt"]) if vkeys else [_key(v_aps[0])]
            for vi, v_ap in enumerate(v_aps):
                ob = B[2 + (qb * nV + vi) % 2]
                rows = v_ap.shape[1]
                kb.mm(ob[0:rows, 0:W], v_ap, Pt, start=d["first"], stop=d["last"], rk=vk + [_key(Pt)], wk=[_key(ob)])
            nfill = 1 if (FILLK and kind == "softmax" and d["n"] % FILLK == 0) else 0
            if nfill and W == 512:
                fb_ = kb.tbank[0][:, :].bitcast(F32)[:, 0:FILLN]
                for _ in range(nfill):
                    kb.op("pe", lambda e: e.matmul(fb_, kb.onesb[:, 0:128], self.Pt[sl][:, 0:FILLN], start=True, stop=True),
                          rk=[], wk=[])
            if not d["last"]:
                return
            for vi in range(nV):
                ob = B[2 + (qb * nV + vi) % 2]
                on = self.onrm[(qb * nV + vi) % 2]
                if kind == "sb":
                    kb.copy("act", on[0:dv, 0:W], ob[0:dv, 0:W])
                else:
                    kb.copy("act", self.osb[0:dv + 1, 0:W], ob[0:dv + 1, 0:W])
                    kb.recip(self.rs[dv:dv + 1, 0:W], self.osb[dv:dv + 1, 0:W])
                    kb.mm(B[4][0:dv, 0:W], kb.ones32[dv:dv + 1, 0:dv], self.rs[dv:dv + 1, 0:W])
                    kb.tt("dve", on[0:dv, 0:W], self.osb[0:dv, 0:W], B[4][0:dv, 0:W], ALU.mult)
                out_dma(qb, vi, on[0:dv, 0:W])

        if not flat:
            return
        for d in flat[0:2]:
            stage_a(d)
        if kind == "sb":
            sb_head(flat[0])
        for i, d in enumerate(flat):
            if i + 2 < len(flat):
                stage_a(flat[i + 2])
            if kind == "sb":
                stage_b1(d)
                if i + 1 < len(flat):
                    sb_head(flat[i + 1])
            stage_b(d)


def memset_rect(kb, tile, p0, p1, c0, c1, val):
    if p1 > p0 and c1 > c0:
        kb.op("pool", lambda e: e.memset(tile[p0:p1, c0:c1], val), w=[tile])


CA_G = 1536
CA_OFF = 1023


def ca_onehot():
    rel = np.arange(CA_G) - CA_OFF
    idx = np.clip(rel, -128, 128) + 128
    oh = np.zeros((257, CA_G), np.float32)
    oh[idx, np.arange(CA_G)] = 1.0
    return oh


def gather_exp_vector(kb, oh_dram, nrows, tab_dram, ncol, n, g_scr):
    kchunks = [(k0, min(128, nrows - k0)) for k0 in range(0, nrows, 128)]
    oh_sb = kb.sb("oh_sb", [128, len(kchunks), 512], F32)
    tab_sb = kb.sb("tab_sb", [128, len(kchunks), ncol], F32)
    gs = kb.sb("g_sb", [16, 512], F32)
    for ki, (k0, kn) in enumerate(kchunks):
        kb.dma("sp", tab_sb[0:kn, ki, :], tab_dram[k0:k0 + kn, :])
    for c0 in range(0, n, 512):
        cw = min(512, n - c0)
        for ki, (k0, kn) in enumerate(kchunks):
            kb.dma("sp", oh_sb[0:kn, ki, 0:cw], oh_dram[k0:k0 + kn, c0:c0 + cw])
        for ki, (k0, kn) in enumerate(kchunks):
            kb.mm(kb.bank[4][0:ncol, 0:cw], tab_sb[0:kn, ki, :], oh_sb[0:kn, ki, 0:cw],
                  start=(ki == 0), stop=(ki == len(kchunks) - 1))
        kb.act(gs[0:ncol, 0:cw], kb.bank[4][0:ncol, 0:cw], AF.Exp)
        kb.dma("sp", g_scr[0:ncol, c0:c0 + cw], gs[0:ncol, 0:cw])


import os
STOP = int(os.environ.get('A_STOP', '99'))
FILLK = int(os.environ.get('FILLK', '1'))
FILLN = int(os.environ.get('FILLN', '512'))


def build_attn1():
    nc = bass.Bass("TRN2", target_bir_lowering=False)
    with ExitStack() as es:
        kb = KB(nc, es)
        hp = kb.din("hp", [SEQ, D])
        hs = kb.din("hs", [DSEQ, D])
        w_p = kb.din("w_p", [D, 384])
        w_s = kb.din("w_s", [D, 3072])
        c_sbk = kb.din("c_sbk", [PAST, 512])
        c_sbv = kb.din("c_sbv", [PAST, 512])
        c_cak = kb.din("c_cak", [512, 512])
        c_cav = kb.din("c_cav", [512, 512])
        ca_oh = kb.din("ca_oh", [257, CA_G])
        ca_tab = kb.din("ca_tab", [257, 9])
        o_sk = kb.dout("o_sk", [SEQ, 64])
        o_sv = kb.dout("o_sv", [SEQ, 64])
        o_ck = kb.dout("o_ck", [512, 64])
        o_cv = kb.dout("o_cv", [512, 64])
        oT_sb = kb.dout("oT_sb", [64, SEQ])
        oT_ca = kb.dout("oT_ca", [64, SEQ])
        s_sk = kb.dout("s_sk", [DSEQ, 512])
        s_sv = kb.dout("s_sv", [DSEQ, 512])
        s_cak = kb.dout("s_cak", [512, 512])
        s_cav = kb.dout("s_cav", [512, 512])
        soT = kb.dout("soT", [1024, DSEQ])
        g_scr = kb.dscr("g_scr", [9, CA_G])
        scr_cqk = kb.dscr("scr_cqk", [64, 2, SEQ], BF16)
        scr_cv = kb.dscr("scr_cv", [SEQ, 64], BF16)

        at = Attn(kb)
        if STOP == -1:
            kb.dma("sp", o_ck[0:128, :], at.utri[:, 0:64].bitcast(F32)[:, 0:32] if False else kb.ones32[:, 0:64])
            kb.P.finish()
            return nc
        gather_exp_vector(kb, ca_oh, 257, ca_tab, 9, CA_G, g_scr)
        if STOP == -2:
            kb.P.finish()
            return nc
        stage = kb.sb("stage", [128, 8, 256], F32)
        BQ = kb.sb("BQ", [128, SEQ], BF16)
        BK = kb.sb("BK", [128, SEQ], BF16)
        BV = kb.sb("BV", [128, NT * 65], BF16)
        sc = 64 ** -0.5
        if STOP == 0:
            kb.P.finish()
            return nc

        xts = [kb.sb("xt%d" % i, [128, D], F32) for i in range(2)]
        hbs = [kb.sb("hb%d" % i, [128, D], BF16) for i in range(2)]
        xt, hb = xts[0], hbs[0]
        hT = [kb.sb("hT%d" % i, [128, 8, 128], BF16) for i in range(2)]
        sp32 = [kb.sb("sp32_%d" % i, [128, 512], F32) for i in range(2)]
        cst = [kb.sb("cst%d" % i, [128, 512], F32) for i in range(2)]
        cb = [kb.sb("cb%d" % i, [128, 512], BF16) for i in range(2)]
        QTs = kb.sb("QTs", [128, 8, DSEQ], BF16)
        Esm = kb.sb("Esm", [128, 5, 8, DSEQ], BF16)
        Msm = kb.sb("Msm", [128, DSEQ], BF16)
        KTs = BQ[:, 0:4 * 2080].rearrange("p (a n) -> p a n", a=4)
        Vs = BK[:, 0:17 * 8 * 65].rearrange("p (t h d) -> p t h d", t=17, h=8)
        spb = BV[:, 0:3072]
        wsg = BV[:, 4096:8192].rearrange("p (k n) -> p k n", k=8)
        KTc = BK[:, 9000:9000 + 4 * 544].rearrange("p (a n) -> p a n", a=4)
        Vc = BK[:, 11200:11200 + 5 * 8 * 65].rearrange("p (t h d) -> p t h d", t=5, h=8)

        kb.dma("sp", xt[0:DSEQ, :], hs)
        kb.copy("dve", hb[0:DSEQ, :], xt[0:DSEQ, :])
        kb.transpose_cols(hT[0], hb, DSEQ, D, kb.tbank[0])
        if STOP == 10:
            kb.P.finish()
            return nc
        for g in range(6):
            for c0 in range(0, 512, 256):
                kb.dma("sp", stage[:, :, :], w_s[:, g * 512 + c0:g * 512 + c0 + 256].rearrange("(k p) n -> p k n", p=128))
                kb.copy("dve", wsg[:, :, c0:c0 + 256], stage[:, :, :])
            bk = kb.bank[g % 2]
            DBG = os.environ.get("DBG", "")
            if "nomm" in DBG:
                if STOP == 11 and g == 1:
                    kb.P.finish()
                    return nc
                continue
            for k in range(8):
                kb.mm(bk[0:DSEQ, :], hT[0][:, k, 0:DSEQ], wsg[:, k, :], start=(k == 0), stop=(k == 7))
            if "nospb" not in DBG:
                kb.copy("dve", spb[0:DSEQ, g * 512:(g + 1) * 512], bk[0:DSEQ, :])
            if g in (1, 2, 4, 5) and "nost" not in DBG:
                s32 = sp32[g % 2]
                kb.copy("dve" if "dvecp" in DBG else "act", s32[0:DSEQ, :], bk[0:DSEQ, :],
                        rk=[_key(bk), "BV"] if "ser" in DBG else None)
                dst = {1: s_sk, 2: s_sv, 4: s_cak[480:512, :], 5: s_cav[480:512, :]}[g]
                if "nodma" not in DBG:
                    kb.dma("pool", dst, s32[0:DSEQ, :])
            if STOP == 11 and g == 1:
                kb.P.finish()
                return nc
        if STOP == 12:
            kb.P.finish()
            return nc
        for src_c, dst_c in ((c_cak, s_cak), (c_cav, s_cav)):
            stv = stage[0:120, :, :].rearrange("p k n -> p (k n)")
            kb.dma("sp", stv, src_c[32:512, :].rearrange("(a b) n -> a (b n)", a=120))
            kb.dma("pool", dst_c[0:480, :].rearrange("(a b) n -> a (b n)", a=120), stv)
        if STOP == 1:
            kb.P.finish()
            return nc
        for a in range(8):
            src0 = (0 if a < 4 else 1536) + (a % 4) * 128
            kb.tr(kb.tbank[1][:, a * DSEQ:(a + 1) * DSEQ], spb[0:DSEQ, src0:src0 + 128])
        kb.evac(QTs[:, :, :], kb.tbank[1][:, 0:8 * DSEQ].rearrange("p (a t) -> p a t", a=8))
        kb.op("pool", lambda e: e.memset(Vs[:, :, :, 64:65], 1.0), w=[BK])
        kb.op("pool", lambda e: e.memset(Vc[:, :, :, 64:65], 1.0), w=[BK])

        def fill_kv(ck_dram, cv_dram, ntile, KTv, Vv, kcol, vcol):
            for j in range(ntile):
                s0, s1 = cst[0], cst[1]
                kb.dma("sp", s0[:, :], ck_dram[j * 128:(j + 1) * 128, :])
                kb.dma("sp", s1[:, :], cv_dram[j * 128:(j + 1) * 128, :])
                kb.copy("pool", cb[0][:, :], s0[:, :])
                tb = kb.tbank[j % 2]
                for a in range(4):
                    kb.tr(tb[:, a * 128:(a + 1) * 128], cb[0][:, a * 128:(a + 1) * 128])
                kb.evac(KTv[:, :, j * 128:(j + 1) * 128], tb[:, 0:512].rearrange("p (a t) -> p a t", a=4))
                kb.copy("dve", Vv[:, j, :, 0:64], s1[:, :].rearrange("p (h d) -> p h d", h=8))
            tb = kb.tbank[ntile % 2]
            for a in range(4):
                kb.tr(tb[:, a * DSEQ:(a + 1) * DSEQ], spb[0:DSEQ, kcol + a * 128:kcol + (a + 1) * 128])
            kb.evac(KTv[:, :, ntile * 128:ntile * 128 + DSEQ], tb[:, 0:4 * DSEQ].rearrange("p (a t) -> p a t", a=4))
            kb.copy("dve", Vv[0:DSEQ, ntile, :, 0:64], spb[0:DSEQ, vcol:vcol + 512].rearrange("p (h d) -> p h d", h=8))

        fill_kv(c_sbk, c_sbv, 16, KTs, Vs, 512, 1024)
        fill_kv(c_cak, c_cav, 4, KTc, Vc, 2048, 2560)
        if STOP == 2:
            kb.P.finish()
            return nc
        kb.op("pool", lambda e: e.affine_select(out=Msm[0:DSEQ, :], in_=kb.onesb[0:DSEQ, 0:DSEQ], pattern=[[1, DSEQ]],
                                                compare_op=ALU.is_gt, fill=0.0, base=0, channel_multiplier=-1),
              r=[kb.onesb], w=[Msm])
        for h in range(8):
            for j in range(5):
                nk = 128 if j < 4 else DSEQ
                base = (128 * j + 480) if j < 4 else 992
                at.build_E(Esm[0:nk, j, h, :], g_scr[h, :], base, nk, DSEQ)
        if STOP == 3:
            kb.P.finish()
            return nc
        for h in range(8):
            pr, pb_ = h // 2, 64 * (h % 2)
            at.run("sb", 1, DSEQ,
                   lambda qb: [(16, DSEQ, Msm[0:DSEQ, :], False)] + [(kt, 128, None, False) for kt in range(15, -1, -1)],
                   lambda kt, pr=pr, pb_=pb_: KTs[pb_:pb_ + 64, pr, kt * 128:kt * 128 + (128 if kt < 16 else DSEQ)],
                   lambda qb, pr=pr, pb_=pb_: QTs[pb_:pb_ + 64, pr, :],
                   lambda kt, h=h: [Vs[0:(128 if kt < 16 else DSEQ), kt, h, 0:64]],
                   sc, lambda qb, vi, on, h=h: kb.dma("pool", soT[h * 64:(h + 1) * 64, :], on))
            at.run("softmax", 1, DSEQ,
                   lambda qb, h=h: [(j, 128 if j < 4 else DSEQ, Esm[0:(128 if j < 4 else DSEQ), j, h, :], False)
                                    for j in range(5)],
                   lambda kt, pr=pr, pb_=pb_: KTc[pb_:pb_ + 64, pr, kt * 128:kt * 128 + (128 if kt < 4 else DSEQ)],
                   lambda qb, pr=pr, pb_=pb_: QTs[pb_:pb_ + 64, 4 + pr, :],
                   lambda kt, h=h: [Vc[0:(128 if kt < 4 else DSEQ), kt, h, 0:65]],
                   sc, lambda qb, vi, on, h=h: kb.dma("pool", soT[512 + h * 64:512 + (h + 1) * 64, :], on))
        kb.P.barrier()
        if STOP == 4:
            kb.P.finish()
            return nc

        wp_b = kb.sb("wp_b", [128, 8, 384], BF16)
        kb.load_w(wp_b, w_p, 8, 384, stage) if False else None
        for c0 in range(0, 384, 256):
            c1 = min(384, c0 + 256)
            kb.dma("sp", stage[:, :, 0:c1 - c0], w_p[:, c0:c1].rearrange("(k p) n -> p k n", p=128))
            kb.copy("dve", wp_b[:, :, c0:c1], stage[:, :, 0:c1 - c0])
        po = [kb.sb("po%d" % i, [128, 384], F32) for i in range(2)]
        pb = [kb.sb("pb%d" % i, [128, 384], BF16) for i in range(2)]
        caT = [kb.sb("caT%d" % i, [64, 2, 128], BF16) for i in range(2)]
        BVv = BV[:, :].rearrange("p (t d) -> p t d", d=65)
        kb.op("pool", lambda e: e.memset(BVv[:, :, 64:65], 1.0), w=[BV])
        kb.P.barrier()
        for t in range(NT):
            sl = t % 2
            r0 = t * 128
            kb.dma("sp", xts[sl][:, :], hp[r0:r0 + 128, :])
            kb.copy("dve", hbs[sl][:, :], xts[sl][:, :])
            kb.transpose_cols(hT[sl], hbs[sl], 128, D, kb.tbank[sl])
            bk = kb.bank[sl]
            for k in range(8):
                kb.mm(bk[:, 0:384], hT[sl][:, k, :], wp_b[:, k, :], start=(k == 0), stop=(k == 7))
            kb.copy("act", po[sl][:, :], bk[:, 0:384])
            kb.copy("dve", pb[sl][:, :], bk[:, 0:384])
            kb.dma("pool", o_sk[r0:r0 + 128, :], po[sl][:, 64:128])
            kb.dma("pool", o_sv[r0:r0 + 128, :], po[sl][:, 128:192])
            if t >= NT - 4:
                rr0 = (t - (NT - 4)) * 128
                kb.dma("pool", o_ck[rr0:rr0 + 128, :], po[sl][:, 256:320])
                kb.dma("pool", o_cv[rr0:rr0 + 128, :], po[sl][:, 320:384])
            tb = kb.tbank[1 - sl]
            for j, c0 in enumerate((0, 64, 192, 256)):
                kb.tr(tb[0:64, j * 128:(j + 1) * 128], pb[sl][:, c0:c0 + 64])
            kb.copy("act", BQ[0:64, r0:r0 + 128], tb[0:64, 0:128], wk=["BQ.%d" % t])
            kb.copy("dve", BK[0:64, r0:r0 + 128], tb[0:64, 128:256], wk=["BK.%d" % t])
            kb.copy("act", caT[sl][:, :, :], tb[0:64, 256:512].rearrange("p (a t) -> p a t", a=2))
            kb.dma("pool", scr_cqk[:, :, r0:r0 + 128], caT[sl][:, :, :])
            kb.copy("act", BVv[:, t, 0:64], pb[sl][:, 128:192], wk=["BV.%d" % t])
            kb.dma("pool", scr_cv[r0:r0 + 128, :], pb[sl][:, 320:384])
        if STOP == 5:
            kb.P.finish()
            return nc

        Msb = kb.sb("Msb", [128, 4, 512], BF16)
        for i in range(4):
            kb.op("pool", lambda e, i=i: e.affine_select(out=Msb[:, i, :], in_=kb.onesb[:, 0:512], pattern=[[1, 512]],
                                                         compare_op=ALU.is_gt, fill=0.0, base=-128 * i,
                                                         channel_multiplier=-1), r=[kb.onesb], w=[Msb])
        at.run("sb", SEQ // 512, 512,
               lambda qb: [(kt, 128, (Msb[:, kt - 4 * qb, :] if kt >= 4 * qb else None), False)
                           for kt in range(4 * qb + 3, -1, -1)],
               lambda kt: BK[0:64, kt * 128:(kt + 1) * 128],
               lambda qb: BQ[0:64, qb * 512:(qb + 1) * 512],
               lambda kt: [BVv[:, kt, 0:64]],
               sc, lambda qb, vi, on: kb.dma("pool", oT_sb[:, qb * 512:(qb + 1) * 512], on),
               kkeys=lambda kt: ["BK.%d" % kt], qkeys=lambda qb: ["BQ.%d" % (4 * qb + i) for i in range(4)],
               vkeys=lambda kt: ["BV.%d" % kt])
        kb.P.barrier()

        kb.dma("sp", BQ[0:64, :], scr_cqk[:, 0, :], wk=["BQc"])
        kb.dma("sp", BK[0:64, :], scr_cqk[:, 1, :], wk=["BKc"])
        for c in range(16):
            kb.dma("sp", BVv[:, c * 8:(c + 1) * 8, 0:64],
                   scr_cv[c * 1024:(c + 1) * 1024, :].rearrange("(t p) d -> p t d", p=128), sem="BVc", wk=["BVc"])
        Eca = kb.sb("Eca", [128, 8, 512], BF16)
        Mt = kb.sb("Mt", [128, 512], BF16)
        for i in range(-4, 4):
            kb.op("pool", lambda e: e.memset(Mt[:, :], 0.0), w=[Mt])
            for a in range(2):
                kc = 2 * i + a
                memset_rect(kb, Mt, 64 * a, 64 * a + 64, 64 * max(0, kc), 64 * min(8, kc + 9), 1.0)
            at.build_E(Eca[:, i + 4, :], g_scr[8, :], 128 * i + 512, 128, 512, mask=Mt[:, :])
        at.run("softmax", SEQ // 512, 512,
               lambda qb: [(4 * qb + i, 128, Eca[:, i + 4, :], False) for i in range(-4, 4) if 4 * qb + i >= 0],
               lambda kt: BK[0:64, kt * 128:(kt + 1) * 128],
               lambda qb: BQ[0:64, qb * 512:(qb + 1) * 512],
               lambda kt: [BVv[:, kt, 0:65]],
               sc, lambda qb, vi, on: kb.dma("pool", oT_ca[:, qb * 512:(qb + 1) * 512], on),
               kkeys=lambda kt: ["BKc"], qkeys=lambda qb: ["BQc"], vkeys=lambda kt: ["BVc"])
        kb.P.finish()
    return nc


T5_G = 1152
T5_OFF = 639


def t5_onehot():
    import jax
    import jax.numpy as jnp
    with jax.default_device(jax.devices("cpu")[0]):
        rel = jnp.arange(T5_G, dtype=jnp.int32) - T5_OFF
        nb, max_exact = 16, 8
        ret = jnp.where(rel > 0, nb, 0)
        n = jnp.abs(rel)
        nf = jnp.maximum(n, 1).astype(jnp.float32)
        large = max_exact + (jnp.log(nf / max_exact) / math.log(128 / max_exact) * (nb - max_exact)).astype(jnp.int32)
        large = jnp.minimum(large, nb - 1)
        b = np.asarray(ret + jnp.where(n < max_exact, n, large))
    oh = np.zeros((32, T5_G), np.float32)
    oh[b, np.arange(T5_G)] = 1.0
    return oh


def rope_table(pos):
    half = 16
    inv = (10000.0 ** (-np.arange(half, dtype=np.float32) / half)).astype(np.float32)
    ang = pos.astype(np.float32)[:, None] * inv[None, :]
    c, s = np.cos(ang).astype(np.float32), np.sin(ang).astype(np.float32)
    return np.ascontiguousarray(np.concatenate([c, c, -s, s], axis=1))


def build_attn0():
    nc = bass.Bass("TRN2", target_bir_lowering=False)
    with ExitStack() as es:
        kb = KB(nc, es)
        xp = kb.din("xp", [SEQ, D])
        xs = kb.din("xs", [DSEQ, D])
        g_mix = kb.din("g_mix", [D])
        q_norm = kb.din("q_norm", [256])
        kv_norm = kb.din("kv_norm", [128])
        w_lat = kb.din("w_lat", [D, 416])
        w_dp = kb.din("w_dp", [D, 256])
        w_dall = kb.din("w_dall", [D, 1536])
        w_uq = kb.din("w_uq", [256, 768])
        w_uk = kb.din("w_uk", [128, 512])
        w_uv = kb.din("w_uv", [128, 512])
        w_uqh = kb.din("w_uqh", [256, 96])
        w_ukvh = kb.din("w_ukvh", [128, 128])
        c_ckv = kb.din("c_ckv", [PAST, 128])
        c_kr = kb.din("c_kr", [PAST, 32])
        c_dk = kb.din("c_dk", [PAST, 512])
        c_dv = kb.din("c_dv", [PAST, 512])
        rope_p = kb.din("rope_p", [SEQ, 64])
        rope_s = kb.din("rope_s", [DSEQ, 64])
        t5_oh = kb.din("t5_oh", [32, T5_G])
        t5_tab = kb.din("t5_tab", [32, 5])
        o_lat = kb.dout("o_lat", [SEQ, 160])
        o_dk = kb.dout("o_dk", [SEQ, 64])
        o_dv = kb.dout("o_dv", [SEQ, 128])
        oT_m = kb.dout("oT_m", [64, SEQ])
        oT_d = kb.dout("oT_d", [128, SEQ])
        s_lat = kb.dout("s_lat", [DSEQ, 160])
        s_dk = kb.dout("s_dk", [DSEQ, 512])
        s_dv = kb.dout("s_dv", [DSEQ, 512])
        soT_m = kb.dout("soT_m", [512, DSEQ])
        soT_d = kb.dout("soT_d", [1024, DSEQ])
        g_scr = kb.dscr("g_scr", [5, T5_G])
        scr_dqk = kb.dscr("scr_dqk", [64, 2, SEQ], BF16)
        scr_dv = kb.dscr("scr_dv", [SEQ, 130], BF16)

        at = Attn(kb, sb=False)
        gather_exp_vector(kb, t5_oh, 32, t5_tab, 5, T5_G, g_scr)
        stage = kb.sb("stage", [128, 8, 128], F32)
        wlat_b = kb.sb("wlat_b", [128, 8, 416], BF16)
        wdp_b = kb.sb("wdp_b", [128, 8, 256], BF16)
        wuq_b = kb.sb("wuq_b", [128, 2, 768], BF16)
        wuk_b = kb.sb("wuk_b", [128, 1, 512], BF16)
        wuv_b = kb.sb("wuv_b", [128, 1, 512], BF16)
        wuqh_b = kb.sb("wuqh_b", [128, 2, 96], BF16)
        wukvh_b = kb.sb("wukvh_b", [128, 1, 128], BF16)
        for dst, src, kc, ncol in ((wlat_b, w_lat, 8, 416), (wdp_b, w_dp, 8, 256), (wuq_b, w_uq, 2, 768),
                                   (wuk_b, w_uk, 1, 512), (wuv_b, w_uv, 1, 512), (wuqh_b, w_uqh, 2, 96),
                                   (wukvh_b, w_ukvh, 1, 128)):
            for c0 in range(0, ncol, 128):
                c1 = min(ncol, c0 + 128)
                kb.dma("sp", stage[:, 0:kc, 0:c1 - c0], src[:, c0:c1].rearrange("(k p) n -> p k n", p=128))
                kb.copy("dve", dst[:, 0:kc, c0:c1], stage[:, 0:kc, 0:c1 - c0])
        gmix_bc = kb.bcast_row("gmix_bc", g_mix, D)
        qn_bc = kb.bcast_row("qn_bc", q_norm, 256)
        kvn_bc = kb.bcast_row("kvn_bc", kv_norm, 128)
        fb = kb.sb("fb", [128, 5], F32)
        kb.dma("sp", fb[:, :], t5_tab[15, :].partition_broadcast(128))
        xt = kb.sb("xt", [128, D], F32)
        hb = kb.sb("hb", [128, D], BF16)
        hT = [kb.sb("hT0", [128, 8, 128], BF16)] * 2
        cs = [kb.sb("cs%d" % i, [128, 64], F32) for i in range(2)]
        ssq = kb.sb("ssq", [128, 4], F32)
        latsb = kb.sb("latsb", [128, 416], F32)
        dpsb = kb.sb("dpsb", [128, 512], F32)
        lat_o = [kb.sb("lat_o%d" % i, [128, 160], F32) for i in range(2)]
        cqn = kb.sb("cqn", [128, 256], BF16)
        ckvb = kb.sb("ckvb", [128, 128], BF16)
        cqT = kb.sb("cqT", [128, 2, 128], BF16)
        ckvT = kb.sb("ckvT", [128, 1, 128], BF16)
        upsb = kb.sb("upsb", [128, 1024], F32)
        junk = upsb
        rA = kb.sb("rA", [128, 32], F32)
        rB = kb.sb("rB", [128, 32], F32)
        qf = kb.sb("qf", [128, 8, 96], BF16)
        kf = kb.sb("kf", [128, 8, 96], BF16)
        dqkb = kb.sb("dqkb", [128, 512], BF16)
        dvt = [kb.sb("dvt%d" % i, [128, 130], BF16) for i in range(2)]
        dT = [kb.sb("dT%d" % i, [64, 2, 128], BF16) for i in range(2)]
        krb = kb.sb("krb", [128, 32], BF16)
        cst = [kb.sb("cst%d" % i, [128, 512], F32) for i in range(2)]
        cb = kb.sb("cb", [128, 512], BF16)
        QTm = kb.sb("QTm", [128, 8, DSEQ], BF16)
        QTd = kb.sb("QTd", [128, 4, DSEQ], BF16)
        Esd = kb.sb("Esd", [128, 2, 4, DSEQ], BF16)
        Mcc = kb.sb("Mcc", [128, 4, 512], BF16)
        Ed = kb.sb("Ed", [128, 5, 512], BF16)
        BQ = kb.sb("BQ", [128, SEQ], BF16)
        BK = kb.sb("BK", [128, SEQ], BF16)
        BV = kb.sb("BV", [128, NT * 130], BF16)
        BVv = BV[:, :].rearrange("p (t d) -> p t d", d=130)
        for i in range(2):
            kb.op("pool", lambda e, i=i: e.memset(dvt[i][:, :], 1.0), w=[dvt[i]])
        kb.op("pool", lambda e: e.memset(Mcc[:, :, :], 1.0), w=[Mcc])
        for i in range(4):
            for a in range(2):
                kc = 2 * i + a
                if kc > 0:
                    kb.op("pool", lambda e, i=i, a=a, kc=kc: e.memset(Mcc[64 * a:64 * a + 64, i, 0:64 * kc], 0.0), w=[Mcc])
        for i in range(-1, 4):
            at.build_E(Ed[:, i + 1, :], g_scr[4, :], 128 * i + 128, 128, 512, mask=(Mcc[:, i, :] if i >= 0 else None))

        def rope(dst, src, tab, nt):
            kb.tt("dve", rA[0:nt, :], src, tab[0:nt, 0:32], ALU.mult)
            sw = src.rearrange("p (a b) -> p a b", a=2)[:, ::-1, :]
            kb.tt("dve", rB[0:nt, :].rearrange("p (a b) -> p a b", a=2), sw,
                  tab[0:nt, 32:64].rearrange("p (a b) -> p a b", a=2), ALU.mult)
            kb.tt("dve", dst, rA[0:nt, :], rB[0:nt, :], ALU.add)

        xt_all = [xt, stage[:, :, :].rearrange("p k n -> p (k n)")]
        hb_all = [hb, cst[0][:, :].bitcast(BF16)]
        hT_all = [hT[0], cst[1][:, :].bitcast(BF16).rearrange("p (k n) -> p k n", k=8)]

        def latent_tile(x_src, tab_src, nt, sl, lat_dst, dbl=False):
            xt = xt_all[sl if dbl else 0]
            hb = hb_all[sl if dbl else 0]
            hT = [hT_all[sl if dbl else 0]] * 2
            kb.dma("sp", xt[0:nt, :], x_src)
            kb.dma("sp", cs[sl][0:nt, :], tab_src)
            kb.rms_scale(xt[0:nt, :], nt, D, ssq[:, 0:1], junk)
            kb.stt(hb[0:nt, :], xt[0:nt, :], ssq[0:nt, 0:1], gmix_bc[0:nt, :], ALU.mult, ALU.mult)
            kb.transpose_cols(hT[sl], hb, nt, D, kb.tbank[0])
            for k in range(8):
                kb.mm(kb.bank[0][0:nt, 0:416], hT[sl][:, k, 0:nt], wlat_b[:, k, :], start=(k == 0), stop=(k == 7))
            kb.copy("act", latsb[0:nt, :], kb.bank[0][0:nt, 0:416])
            lo = lat_o[sl]
            kb.rms_scale(latsb[0:nt, 0:256], nt, 256, ssq[:, 1:2], junk)
            kb.stt(cqn[0:nt, :], latsb[0:nt, 0:256], ssq[0:nt, 1:2], qn_bc[0:nt, :], ALU.mult, ALU.mult)
            kb.rms_scale(latsb[0:nt, 256:384], nt, 128, ssq[:, 2:3], junk)
            kb.stt(lo[0:nt, 0:128], latsb[0:nt, 256:384], ssq[0:nt, 2:3], kvn_bc[0:nt, :], ALU.mult, ALU.mult)
            kb.copy("dve", ckvb[0:nt, :], lo[0:nt, 0:128])
            rope(lo[0:nt, 128:160], latsb[0:nt, 384:416], cs[sl], nt)
            kb.dma("pool", lat_dst, lo[0:nt, :])
            kb.transpose_cols(cqT, cqn, nt, 256, kb.tbank[1])
            kb.transpose_cols(ckvT, ckvb, nt, 128, kb.tbank[1])

        sc_m = 96 ** -0.5
        sc_d = 64 ** -0.5

        spb = BV[:, 0:1536]
        wsg = BV[:, 2048:6144].rearrange("p (k n) -> p k n", k=8)
        KTm = BQ[:, 0:4 * 2080].rearrange("p (a n) -> p a n", a=4)
        KTd = BV[:, 6400:6400 + 4 * 2080].rearrange("p (a n) -> p a n", a=4)
        Vm = BK[:, 0:17 * 4 * 65].rearrange("p (t h d) -> p t h d", t=17, h=4)
        Vd = BK[:, 4480:4480 + 17 * 4 * 130].rearrange("p (t h d) -> p t h d", t=17, h=4)
        latent_tile(xs, rope_s, DSEQ, 0, s_lat)
        los = lat_o[0]
        for g in range(3):
            for c0 in range(0, 512, 128):
                kb.dma("sp", stage[:, :, :], w_dall[:, g * 512 + c0:g * 512 + c0 + 128].rearrange("(k p) n -> p k n", p=128))
                kb.copy("dve", wsg[:, :, c0:c0 + 128], stage[:, :, :])
            bk = kb.bank[1]
            for k in range(8):
                kb.mm(bk[0:DSEQ, :], hT[0][:, k, 0:DSEQ], wsg[:, k, :], start=(k == 0), stop=(k == 7))
            kb.copy("dve", spb[0:DSEQ, g * 512:(g + 1) * 512], bk[0:DSEQ, :])
            if g > 0:
                kb.copy("act", dpsb[0:DSEQ, :], bk[0:DSEQ, :])
                kb.dma("pool", s_dk if g == 1 else s_dv, dpsb[0:DSEQ, :])
        for j0, w in ((0, 512), (512, 256)):
            for j in range(2):
                kb.mm(kb.bank[2][0:DSEQ, 0:w], cqT[:, j, 0:DSEQ], wuq_b[:, j, j0:j0 + w], start=(j == 0), stop=(j == 1))
            kb.copy("act", upsb[0:DSEQ, j0:j0 + w], kb.bank[2][0:DSEQ, 0:w])
        for h in range(8):
            kb.copy("pool", qf[0:DSEQ, h, 0:64], upsb[0:DSEQ, h * 96:h * 96 + 64])
            rope(qf[0:DSEQ, h, 64:96], upsb[0:DSEQ, h * 96 + 64:h * 96 + 96], cs[0], DSEQ)
        for h in range(8):
            kb.tr(kb.tbank[0][0:96, h * DSEQ:(h + 1) * DSEQ], qf[0:DSEQ, h, :])
        kb.evac(QTm[0:96, :, :], kb.tbank[0][0:96, 0:8 * DSEQ].rearrange("p (a t) -> p a t", a=8))
        for h in range(4):
            kb.tr(kb.tbank[1][:, h * DSEQ:(h + 1) * DSEQ], spb[0:DSEQ, h * 128:(h + 1) * 128])
        kb.evac(QTd[:, :, :], kb.tbank[1][:, 0:4 * DSEQ].rearrange("p (a t) -> p a t", a=4))
        kb.op("pool", lambda e: e.memset(Vd[:, :, :, :], 1.0), w=[BK])
        for j in range(17):
            nk = 128 if j < 16 else DSEQ
            if j < 16:
                kb.dma("sp", cst[0][:, :], c_dk[j * 128:(j + 1) * 128, :])
                kb.dma("sp", cst[1][:, :], c_dv[j * 128:(j + 1) * 128, :])
                kb.copy("pool", cb[:, :], cst[0][:, :])
                ksrc, vsrc = cb, cst[1]
                kcol = vcol = 0
            else:
                ksrc, vsrc, kcol, vcol = spb, spb, 512, 1024
            tb = kb.tbank[j % 2]
            for a in range(4):
                kb.tr(tb[:, a * nk:(a + 1) * nk], ksrc[0:nk, kcol + a * 128:kcol + (a + 1) * 128])
            kb.evac(KTd[:, :, j * 128:j * 128 + nk], tb[:, 0:4 * nk].rearrange("p (a t) -> p a t", a=4))
            kb.copy("dve", Vd[0:nk, j, :, :].rearrange("p h (a d) -> p h a d", a=2)[:, :, :, 0:64],
                    vsrc[0:nk, vcol:vcol + 512].rearrange("p (h a d) -> p h a d", h=4, a=2))
        for h in range(4):
            at.build_E(Esd[:, 0, h, :], g_scr[h, :], 480, 128, DSEQ)
            at.build_E(Esd[0:DSEQ, 1, h, :], g_scr[h, :], 608, DSEQ, DSEQ)
        for h in range(4):
            for m in range(2):
                at.run("softmax", 1, DSEQ,
                       lambda qb, h=h: [(kt, 128, None, True) for kt in range(15)] +
                                       [(15, 128, Esd[:, 0, h, :], False), (16, DSEQ, Esd[0:DSEQ, 1, h, :], False)],
                       lambda kt, h=h, m=m: KTd[64 * m:64 * m + 64, h, kt * 128:kt * 128 + (128 if kt < 16 else DSEQ)],
                       lambda qb, h=h, m=m: QTd[64 * m:64 * m + 64, h, :],
                       lambda kt, h=h: [Vd[0:(128 if kt < 16 else DSEQ), kt, h, 0:65],
                                        Vd[0:(128 if kt < 16 else DSEQ), kt, h, 65:130]],
                       sc_d, lambda qb, vi, on, h=h, m=m: kb.dma(
                           "pool", soT_d[m * 512 + h * 128 + vi * 64:m * 512 + h * 128 + (vi + 1) * 64, :], on),
                       far_bias=fb[:, h:h + 1])
        for hg in range(2):
            kb.op("pool", lambda e: e.memset(Vm[:, :, :, 64:65], 1.0), w=[BK])
            for j in range(17):
                nk = 128 if j < 16 else DSEQ
                if j < 16:
                    kb.dma("sp", cst[0][:, 0:128], c_ckv[j * 128:(j + 1) * 128, :])
                    kb.dma("sp", cst[1][:, 0:32], c_kr[j * 128:(j + 1) * 128, :])
                    kb.copy("pool", ckvb[:, :], cst[0][:, 0:128])
                    kb.copy("pool", krb[:, :], cst[1][:, 0:32])
                else:
                    kb.copy("pool", ckvb[0:nk, :], los[0:nk, 0:128])
                    kb.copy("pool", krb[0:nk, :], los[0:nk, 128:160])
                kb.transpose_cols(ckvT, ckvb, nk, 128, kb.tbank[1])
                kb.mm(kb.bank[0][0:nk, 0:256], ckvT[:, 0, 0:nk], wuk_b[:, 0, hg * 256:(hg + 1) * 256])
                kb.mm(kb.bank[1][0:nk, 0:256], ckvT[:, 0, 0:nk], wuv_b[:, 0, hg * 256:(hg + 1) * 256])
                kb.copy("act", kf[0:nk, 0:4, 0:64], kb.bank[0][0:nk, 0:256].rearrange("p (h d) -> p h d", h=4))
                for hh in range(4):
                    kb.copy("pool", kf[0:nk, hh, 64:96], krb[0:nk, :])
                kb.copy("dve", Vm[0:nk, j, :, 0:64], kb.bank[1][0:nk, 0:256].rearrange("p (h d) -> p h d", h=4))
                tb = kb.tbank[0]
                for hh in range(4):
                    kb.tr(tb[0:96, hh * nk:(hh + 1) * nk], kf[0:nk, hh, :])
                kb.evac(KTm[0:96, :, j * 128:j * 128 + nk], tb[0:96, 0:4 * nk].rearrange("p (a t) -> p a t", a=4))
            for hh in range(4):
                h = hg * 4 + hh
                at.run("softmax", 1, DSEQ,
                       lambda qb: [(kt, 128 if kt < 16 else DSEQ, None, False) for kt in range(17)],
                       lambda kt, hh=hh: KTm[0:96, hh, kt * 128:kt * 128 + (128 if kt < 16 else DSEQ)],
                       lambda qb, h=h: QTm[0:96, h, :],
                       lambda kt, hh=hh: [Vm[0:(128 if kt < 16 else DSEQ), kt, hh, 0:65]],
                       sc_m, lambda qb, vi, on, h=h: kb.dma("pool", soT_m[h * 64:(h + 1) * 64, :], on))
        kb.P.barrier()
        if STOP == 4:
            kb.P.finish()
            return nc

        kb.op("pool", lambda e: e.memset(BVv[:, :, 64:65], 1.0), w=[BV])
        kb.P.barrier()
        for t in range(NT):
            sl = t % 2
            r0 = t * 128
            latent_tile(xp[r0:r0 + 128, :], rope_p[r0:r0 + 128, :], 128, sl, o_lat[r0:r0 + 128, :], dbl=True)
            lo = lat_o[sl]
            for k in range(8):
                kb.mm(kb.bank[1][:, 0:256], hT_all[sl][:, k, :], wdp_b[:, k, :], start=(k == 0), stop=(k == 7))
            kb.copy("dve", dpsb[:, 0:256], kb.bank[1][:, 0:256])
            kb.dma("pool", o_dk[r0:r0 + 128, :], dpsb[:, 64:128])
            kb.dma("pool", o_dv[r0:r0 + 128, :], dpsb[:, 128:256])
            kb.copy("dve", dqkb[:, 0:128], dpsb[:, 0:128])
            kb.copy("act", dvt[sl][:, :].rearrange("p (a d) -> p a d", a=2)[:, :, 0:64],
                    dpsb[:, 128:256].rearrange("p (a d) -> p a d", a=2))
            kb.dma("pool", scr_dv[r0:r0 + 128, :], dvt[sl][:, :])
            for j in range(2):
                kb.mm(kb.bank[2][:, 0:96], cqT[:, j, :], wuqh_b[:, j, :], start=(j == 0), stop=(j == 1))
            kb.mm(kb.bank[2][:, 96:224], ckvT[:, 0, :], wukvh_b[:, 0, :])
            kb.copy("act", upsb[:, 0:224], kb.bank[2][:, 0:224])
            kb.copy("dve", qf[:, 0, 0:64], upsb[:, 0:64])
            rope(qf[:, 0, 64:96], upsb[:, 64:96], cs[sl], 128)
            kb.copy("act", kf[:, 0, 0:64], upsb[:, 96:160])
            kb.copy("dve", kf[:, 0, 64:96], lo[:, 128:160])
            kb.copy("dve", BVv[:, t, 0:64], upsb[:, 160:224], wk=["BV.%d" % t])
            tb = kb.tbank[1]
            kb.tr(tb[0:96, 0:128], qf[:, 0, :])
            kb.tr(tb[0:96, 128:256], kf[:, 0, :])
            kb.tr(tb[0:64, 256:384], dqkb[:, 0:64])
            kb.tr(tb[0:64, 384:512], dqkb[:, 64:128])
            kb.copy("act", BQ[0:96, r0:r0 + 128], tb[0:96, 0:128], wk=["BQ.%d" % t])
            kb.copy("dve", BK[0:96, r0:r0 + 128], tb[0:96, 128:256], wk=["BK.%d" % t])
            kb.copy("act", dT[sl][:, :, :], tb[0:64, 256:512].rearrange("p (a t) -> p a t", a=2))
            kb.dma("pool", scr_dqk[:, :, r0:r0 + 128], dT[sl][:, :, :])
        if STOP == 5:
            kb.P.finish()
            return nc
        at.run("softmax", SEQ // 512, 512,
               lambda qb: [(kt, 128, (Mcc[:, kt - 4 * qb, :] if kt >= 4 * qb else None), False)
                           for kt in range(4 * qb + 4)],
               lambda kt: BK[0:96, kt * 128:(kt + 1) * 128],
               lambda qb: BQ[0:96, qb * 512:(qb + 1) * 512],
               lambda kt: [BVv[:, kt, 0:65]],
               sc_m, lambda qb, vi, on: kb.dma("pool", oT_m[:, qb * 512:(qb + 1) * 512], on),
               kkeys=lambda kt: ["BK.%d" % kt], qkeys=lambda qb: ["BQ.%d" % (4 * qb + i) for i in range(4)],
               vkeys=lambda kt: ["BV.%d" % kt])
        kb.P.barrier()
        kb.dma("sp", BQ[0:64, :], scr_dqk[:, 0, :], wk=["BQc"])
        kb.dma("sp", BK[0:64, :], scr_dqk[:, 1, :], wk=["BKc"])
        for c in range(32):
            kb.dma("sp", BVv[:, c * 4:(c + 1) * 4, :],
                   scr_dv[c * 512:(c + 1) * 512, :].rearrange("(t p) d -> p t d", p=128), sem="BVc", wk=["BVc"])
        at.run("softmax", SEQ // 512, 512,
               lambda qb: [(kt, 128, (Ed[:, kt - 4 * qb + 1, :] if kt >= 4 * qb - 1 else None), kt < 4 * qb - 1)
                           for kt in range(4 * qb + 4)],
               lambda kt: BK[0:64, kt * 128:(kt + 1) * 128],
               lambda qb: BQ[0:64, qb * 512:(qb + 1) * 512],
               lambda kt: [BVv[:, kt, 0:65], BVv[:, kt, 65:130]],
               sc_d, lambda qb, vi, on: kb.dma("pool", oT_d[vi * 64:(vi + 1) * 64, qb * 512:(qb + 1) * 512], on),
               far_bias=fb[:, 4:5], kkeys=lambda kt: ["BKc"], qkeys=lambda qb: ["BQc"], vkeys=lambda kt: ["BVc"])
        kb.P.finish()
    return nc


_PROGS = {}
_CONSTS = {}


def _prog(name):
    if name not in _PROGS:
        _PROGS[name] = {"a0": build_attn0, "a1": build_attn1, "b0": lambda: build_dense(0),
                        "b1": lambda: build_dense(1)}[name]()
    return _PROGS[name]


def _const(name):
    if name not in _CONSTS:
        _CONSTS[name] = {"t5_oh": t5_onehot, "ca_oh": ca_onehot,
                         "rope_p": lambda: rope_table(np.arange(SEQ)),
                         "rope_s": lambda: rope_table(PAST + np.arange(DSEQ))}[name]()
    return _CONSTS[name]


def _run(name, in_maps):
    res = run_bass_kernel_spmd(_prog(name), in_maps, core_ids=list(range(NCORES)))
    return res.results


def _c(a):
    return np.ascontiguousarray(a, dtype=np.float32)


def kernel(x_prompt, x_sample, cache_mla_ckv, cache_mla_krope, cache_diff_k, cache_diff_v,
           cache_sb_k, cache_sb_v, cache_ca_k, cache_ca_v,
           norm_mix, norm_ff, w_in_even, mla_q_norm, mla_kv_norm, mla_w_uq, mla_w_ukv,
           diff_lambda_vecs, diff_subln, t5_bias, w_out_even, w_in_odd, ca_rel_bias, w_out_odd,
           w_ff1, w_ff2, final_norm):
    f = lambda a: np.asarray(a, dtype=np.float32)
    x_prompt, x_sample = f(x_prompt), f(x_sample)
    xp = _c(x_prompt[0])
    w_in0 = f(w_in_even)[0]
    w_uq = f(mla_w_uq)[0]
    wk = f(mla_w_ukv)[0].reshape(128, 8, 128)
    t5 = f(t5_bias)
    maps = []
    for c in range(NCORES):
        hd, m = c // 2, c % 2
        dpc = np.concatenate([np.arange(416 + hd * 128 + m * 64, 416 + hd * 128 + m * 64 + 64),
                              np.arange(928 + hd * 128 + m * 64, 928 + hd * 128 + m * 64 + 64),
                              np.arange(1440 + hd * 128, 1440 + hd * 128 + 128)])
        maps.append(dict(
            xp=xp, xs=_c(x_sample[c]), g_mix=_c(f(norm_mix)[0]), q_norm=_c(f(mla_q_norm)[0]),
            kv_norm=_c(f(mla_kv_norm)[0]), w_lat=_c(w_in0[:, 0:416]), w_dp=_c(w_in0[:, dpc]),
            w_dall=_c(w_in0[:, 416:]), w_uq=_c(w_uq), w_uk=_c(wk[:, :, 0:64].reshape(128, 512)),
            w_uv=_c(wk[:, :, 64:128].reshape(128, 512)), w_uqh=_c(w_uq[:, c * 96:(c + 1) * 96]),
            w_ukvh=_c(wk[:, c, :]), c_ckv=_c(f(cache_mla_ckv)[0, c]), c_kr=_c(f(cache_mla_krope)[0, c]),
            c_dk=_c(f(cache_diff_k)[0, c].reshape(PAST, 512)), c_dv=_c(f(cache_diff_v)[0, c].reshape(PAST, 512)),
            rope_p=_const("rope_p"), rope_s=_const("rope_s"), t5_oh=_const("t5_oh"),
            t5_tab=_c(np.concatenate([t5, t5[:, hd:hd + 1]], axis=1))))
    ra = _run("a0", maps)
    p_ckv = _c(ra[0]["o_lat"][:, 0:128]).reshape(1, 1, SEQ, 128)
    p_krope = _c(ra[0]["o_lat"][:, 128:160]).reshape(1, 1, SEQ, 32)
    p_dk = np.zeros((1, 1, SEQ, 4, 128), np.float32)
    p_dv = np.zeros((1, 1, SEQ, 4, 128), np.float32)
    s_ckv = np.zeros((1, 8, DSEQ, 128), np.float32)
    s_krope = np.zeros((1, 8, DSEQ, 32), np.float32)
    s_dk = np.zeros((1, 8, DSEQ, 4, 128), np.float32)
    s_dv = np.zeros((1, 8, DSEQ, 4, 128), np.float32)
    for c in range(NCORES):
        hd, m = c // 2, c % 2
        p_dk[0, 0, :, hd, m * 64:(m + 1) * 64] = ra[c]["o_dk"]
        if m == 0:
            p_dv[0, 0, :, hd, :] = ra[c]["o_dv"]
        s_ckv[0, c] = ra[c]["s_lat"][:, 0:128]
        s_krope[0, c] = ra[c]["s_lat"][:, 128:160]
        s_dk[0, c] = ra[c]["s_dk"].reshape(DSEQ, 4, 128)
        s_dv[0, c] = ra[c]["s_dv"].reshape(DSEQ, 4, 128)
    maps = []
    for r in range(NCORES):
        sl = slice(r * TOK, (r + 1) * TOK)
        mix = np.empty((1536, TB), np.float32)
        for h in range(8):
            mix[h * 64:(h + 1) * 64, :TOK] = ra[h]["oT_m"][:, sl]
        mix[0:512, TOK:] = ra[r]["soT_m"]
        for hd in range(4):
            for m in range(2):
                mix[512 + m * 512 + hd * 128:512 + m * 512 + (hd + 1) * 128, :TOK] = ra[2 * hd + m]["oT_d"][:, sl]
        mix[512:1536, TOK:] = ra[r]["soT_d"]
        maps.append(dict(x_in=_c(np.concatenate([x_prompt[0, sl], x_sample[r]], axis=0)), mixT=mix,
                         w_out=_c(f(w_out_even)[0]), w_ff1=_c(f(w_ff1)[0]), w_ff2=_c(f(w_ff2)[0]),
                         g_ff=_c(f(norm_ff)[0]), g_next=_c(f(norm_mix)[1]),
                         lamv=_c(f(diff_lambda_vecs)[0].reshape(256)), subln=_c(f(diff_subln)[0])))
    rb = _run("b0", maps)
    hp = _c(np.concatenate([rb[r]["hn_out"][:TOK] for r in range(NCORES)], axis=0))
    w_in1 = f(w_in_odd)[0]
    ca = f(ca_rel_bias)[0]
    maps = []
    for c in range(NCORES):
        cols = np.concatenate([np.arange(g * 512 + c * 64, g * 512 + (c + 1) * 64) for g in range(6)])
        maps.append(dict(hp=hp, hs=_c(rb[c]["hn_out"][TOK:]), w_p=_c(w_in1[:, cols]), w_s=_c(w_in1),
                         c_sbk=_c(f(cache_sb_k)[0, c].reshape(PAST, 512)), c_sbv=_c(f(cache_sb_v)[0, c].reshape(PAST, 512)),
                         c_cak=_c(f(cache_ca_k)[0, c].reshape(512, 512)), c_cav=_c(f(cache_ca_v)[0, c].reshape(512, 512)),
                         ca_oh=_const("ca_oh"), ca_tab=_c(np.concatenate([ca, ca[:, c:c + 1]], axis=1))))
    rc = _run("a1", maps)
    p_sbk = np.zeros((1, 1, SEQ, 8, 64), np.float32)
    p_sbv = np.zeros((1, 1, SEQ, 8, 64), np.float32)
    p_cak = np.zeros((1, 1, 512, 8, 64), np.float32)
    p_cav = np.zeros((1, 1, 512, 8, 64), np.float32)
    s_sbk = np.zeros((1, 8, DSEQ, 8, 64), np.float32)
    s_sbv = np.zeros((1, 8, DSEQ, 8, 64), np.float32)
    s_cak = np.zeros((1, 8, 512, 8, 64), np.float32)
    s_cav = np.zeros((1, 8, 512, 8, 64), np.float32)
    for c in range(NCORES):
        p_sbk[0, 0, :, c, :] = rc[c]["o_sk"]
        p_sbv[0, 0, :, c, :] = rc[c]["o_sv"]
        p_cak[0, 0, :, c, :] = rc[c]["o_ck"]
        p_cav[0, 0, :, c, :] = rc[c]["o_cv"]
        s_sbk[0, c] = rc[c]["s_sk"].reshape(DSEQ, 8, 64)
        s_sbv[0, c] = rc[c]["s_sv"].reshape(DSEQ, 8, 64)
        s_cak[0, c] = rc[c]["s_cak"].reshape(512, 8, 64)
        s_cav[0, c] = rc[c]["s_cav"].reshape(512, 8, 64)
    maps = []
    for r in range(NCORES):
        sl = slice(r * TOK, (r + 1) * TOK)
        mix = np.empty((1024, TB), np.float32)
        for h in range(8):
            mix[h * 64:(h + 1) * 64, :TOK] = rc[h]["oT_sb"][:, sl]
            mix[512 + h * 64:512 + (h + 1) * 64, :TOK] = rc[h]["oT_ca"][:, sl]
        mix[:, TOK:] = rc[r]["soT"]
        maps.append(dict(x_in=_c(rb[r]["x_out"]), mixT=mix, w_out=_c(f(w_out_odd)[0]), w_ff1=_c(f(w_ff1)[1]),
                         w_ff2=_c(f(w_ff2)[1]), g_ff=_c(f(norm_ff)[1]), g_next=_c(f(final_norm))))
    rd = _run("b1", maps)
    y_prompt = _c(np.concatenate([rd[r]["hn_out"][:TOK] for r in range(NCORES)], axis=0)).reshape(1, SEQ, D)
    y_sample = _c(np.stack([rd[r]["hn_out"][TOK:] for r in range(NCORES)], axis=0))
    return (y_prompt, y_sample, p_ckv, p_krope, p_dk, p_dv, p_sbk, p_sbv, p_cak, p_cav,
            s_ckv, s_krope, s_dk, s_dv, s_sbk, s_sbv, s_cak, s_cav)
```

```python
import math
import os
import numpy as np
from contextlib import ExitStack
import concourse.bass as bass
import concourse.mybir as mybir
from concourse.bass_utils import run_bass_kernel_spmd

F32 = mybir.dt.float32
BF16 = mybir.dt.bfloat16
AF = mybir.ActivationFunctionType
ALU = mybir.AluOpType

NCORES = 8
D = 1024
SEQ = 16384
NT = SEQ // 128
DSEQ = 32
PAST = 2048
EPS = 1e-6
TOK = SEQ // NCORES
TB = TOK + DSEQ


class Res:
    __slots__ = ("w", "r")

    def __init__(self):
        self.w = None
        self.r = []


class Prog:
    ENG = ("pe", "act", "dve", "pool", "sp")

    def __init__(self, nc, es):
        self.nc = nc
        self.es = es
        self.ops = {e: [] for e in self.ENG}
        self.sems = {}
        self.val = {}
        self.waited = {e: {} for e in self.ENG}
        self.res = {}
        for e in ("pe", "act", "dve", "pool"):
            self._sem("E_" + e)

    def _sem(self, name):
        if name not in self.sems:
            self.sems[name] = self.es.enter_context(self.nc.semaphore(name))
            self.val[name] = 0
        return self.sems[name]

    def _r(self, key):
        r = self.res.get(key)
        if r is None:
            r = self.res[key] = Res()
        return r

    def _deps(self, eng, reads, writes):
        need = {}

        def add(d):
            if d is not None and need.get(d[0], 0) < d[1]:
                need[d[0]] = d[1]

        for k in reads:
            add(self._r(k).w)
        for k in writes:
            r = self._r(k)
            add(r.w)
            for d in r.r:
                add(d)
        out = []
        wd = self.waited[eng]
        own = "E_" + eng
        for s, v in need.items():
            if s == own and eng == "pe":
                continue
            if wd.get(s, 0) >= v:
                continue
            wd[s] = v
            out.append((s, v))
        return out

    def _record(self, sname, v, reads, writes):
        for k in reads:
            r = self._r(k)
            r.r = [d for d in r.r if d[0] != sname]
            r.r.append((sname, v))
        for k in writes:
            r = self._r(k)
            r.w = (sname, v)
            r.r = []

    def op(self, eng, fn, reads=(), writes=()):
        waits = self._deps(eng, reads, writes)
        sname = "E_" + eng
        self.val[sname] += 1
        self._record(sname, self.val[sname], reads, writes)
        self.ops[eng].append((waits, fn, sname, 1))

    def dma(self, queue, out, in_, sem, reads=(), writes=(), **kw):
        waits = self._deps(queue, reads, writes)
        sname = "D_" + sem
        self._sem(sname)
        self.val[sname] += 16
        self._record(sname, self.val[sname], reads, writes)
        self.ops[queue].append((waits, lambda e: e.dma_start(out=out, in_=in_, **kw), sname, 16))

    def barrier(self):
        cur = [(s, v) for s, v in self.val.items() if v > 0]
        for e in self.ENG:
            waits = []
            for s, v in cur:
                if self.waited[e].get(s, 0) < v:
                    self.waited[e][s] = v
                    waits.append((s, v))
            if waits:
                self.ops[e].append((waits, None, None, 0))

    def finish(self):
        self.flush()

    def flush(self):
        nc = self.nc
        final = [(s, v) for s, v in self.val.items() if v > 0]
        engmap = {"pe": "tensor", "act": "scalar", "dve": "vector", "pool": "gpsimd", "sp": "sync"}
        with nc.Block() as block:
            for e, attr in engmap.items():
                def body(eng, ops=self.ops[e], last=(e == "sp")):
                    for waits, fn, sname, inc in ops:
                        for s, v in waits:
                            eng.wait_ge(self.sems[s], v)
                        if fn is not None:
                            fn(eng).then_inc(self.sems[sname], inc)
                    if last:
                        for s, v in final:
                            eng.wait_ge(self.sems[s], v)

                getattr(block, attr)(body)
        for e in self.ENG:
            self.ops[e] = []
            for s, v in final:
                self.waited[e][s] = v

    def collective(self, kind, ins, outs):
        nc = self.nc
        self._sem("CC")
        self.val["CC"] += 1
        v = self.val["CC"]
        with nc.Block() as block:
            def body(g):
                g.collective_compute(kind, ALU.bypass, replica_groups=[list(range(NCORES))],
                                     ins=[ins.opt()], outs=[outs.opt()]).then_inc(self.sems["CC"])
                g.wait_ge(self.sems["CC"], v)
            block.gpsimd(body)
        for e in self.ENG:
            self.waited[e]["CC"] = v


PSUM_KEYS = {"b0", "b1", "b2", "b3", "b4", "b5", "t0", "t1"}


def _key(ap):
    return getattr(ap, "tensor", ap).name


class KB:
    def __init__(self, nc, es):
        self.nc = nc
        self.es = es
        self.P = Prog(nc, es)
        self.bank = [es.enter_context(nc.psum_tensor("b%d" % i, [128, 512], F32)) for i in range(6)]
        self.tbank = [es.enter_context(nc.psum_tensor("t%d" % i, [128, 1024], BF16)) for i in range(2)]
        self.cnt = 0
        self.ident = self.sb("ident", [128, 128], BF16)
        self.ones32 = self.sb("ones32", [128, 128], F32)
        self.onesb = self.sb("onesb", [128, 512], BF16)
        tmp = self.sb("ident32", [128, 128], F32)
        self.op("pool", lambda e: e.memset(self.ones32[:], 1.0), w=[self.ones32])
        self.op("pool", lambda e: e.memset(self.onesb[:], 1.0), w=[self.onesb])
        self.op("pool", lambda e: e.affine_select(out=tmp[:], in_=self.ones32[:], pattern=[[1, 128]],
                                                  compare_op=ALU.is_equal, fill=0.0, base=0,
                                                  channel_multiplier=-1), r=[self.ones32], w=[tmp])
        self.copy("dve", self.ident[:], tmp[:])

    def sb(self, name, shape, dt, es=None):
        nb = int(np.prod(shape[1:])) * (4 if dt == F32 else 2)
        self.sbuf_bytes = getattr(self, "sbuf_bytes", 0) + ((nb + 31) // 32) * 32
        assert self.sbuf_bytes <= 186 * 1024, ("SBUF budget exceeded", name, self.sbuf_bytes)
        return (es or self.es).enter_context(self.nc.sbuf_tensor(name, shape, dt))

    def din(self, name, shape, dt=F32):
        return self.nc.dram_tensor(name, list(shape), dt, kind="ExternalInput").ap()

    def dout(self, name, shape, dt=F32):
        return self.nc.dram_tensor(name, list(shape), dt, kind="ExternalOutput").ap()

    def dscr(self, name, shape, dt=F32):
        return self.nc.dram_tensor(name, list(shape), dt, kind="Internal").ap()

    def rr(self):
        self.cnt += 1
        return self.cnt

    def op(self, eng, fn, r=(), w=(), rk=None, wk=None):
        reads = rk if rk is not None else [_key(a) if not isinstance(a, str) else a for a in r]
        writes = wk if wk is not None else [_key(a) if not isinstance(a, str) else a for a in w]
        writes = list(writes) + [k for k in reads if k in PSUM_KEYS and k not in writes]
        self.P.op(eng, fn, reads, writes)

    def dma(self, q, out, in_, sem=None, rk=None, wk=None, **kw):
        if q == "pool" and os.environ.get("NO_POOL_DMA", "1") == "1":
            q = "act"
        reads = rk if rk is not None else [_key(in_)]
        writes = wk if wk is not None else [_key(out)]
        if sem is None:
            sem = _key(out) if out.tensor.__class__.__name__.startswith("SB") else "st_" + _key(out)
        self.P.dma(q, out, in_, sem, reads, writes, **kw)

    def act(self, out, in_, func, scale=1.0, bias=0.0, accum=None, rk=None, wk=None):
        r = [in_] + ([bias] if not isinstance(bias, float) else []) + ([scale] if not isinstance(scale, float) else [])
        w = [out] + ([accum] if accum is not None else [])
        kw = {}
        if accum is not None:
            kw["accum_out"] = accum
        self.op("act", lambda e: e.activation(out=out, in_=in_, func=func, scale=scale, bias=bias, **kw),
                r=r, w=w, rk=rk, wk=wk)

    def copy(self, eng, out, in_, rk=None, wk=None):
        if eng == "act":
            self.op("act", lambda e: e.activation(out=out, in_=in_, func=AF.Identity), r=[in_], w=[out], rk=rk, wk=wk)
        else:
            self.op(eng, lambda e: e.tensor_copy(out=out, in_=in_), r=[in_], w=[out], rk=rk, wk=wk)

    def tt(self, eng, out, in0, in1, op, rk=None, wk=None):
        self.op(eng, lambda e: e.tensor_tensor(out=out, in0=in0, in1=in1, op=op), r=[in0, in1], w=[out], rk=rk, wk=wk)

    def ts(self, eng, out, in0, s1, s2, op0, op1=None, rk=None, wk=None):
        r = [in0] + [s for s in (s1, s2) if s is not None and not isinstance(s, (int, float))]
        if op1 is None:
            self.op(eng, lambda e: e.tensor_scalar(out=out, in0=in0, scalar1=s1, scalar2=None, op0=op0),
                    r=r, w=[out], rk=rk, wk=wk)
        else:
            self.op(eng, lambda e: e.tensor_scalar(out=out, in0=in0, scalar1=s1, scalar2=s2, op0=op0, op1=op1),
                    r=r, w=[out], rk=rk, wk=wk)

    def stt(self, out, in0, scalar, in1, op0, op1, rk=None, wk=None):
        r = [in0, in1] + ([scalar] if not isinstance(scalar, (int, float)) else [])
        self.op("dve", lambda e: e.scalar_tensor_tensor(out=out, in0=in0, scalar=scalar, in1=in1, op0=op0, op1=op1),
                r=r, w=[out], rk=rk, wk=wk)

    def recip(self, out, in_, rk=None, wk=None):
        self.op("dve", lambda e: e.reciprocal(out=out, in_=in_), r=[in_], w=[out], rk=rk, wk=wk)

    def mm(self, out, lhsT, rhs, start=True, stop=True, rk=None, wk=None, skip=False):
        self.op("pe", lambda e: e.matmul(out, lhsT, rhs, start=start, stop=stop, skip_group_check=skip),
                r=[lhsT, rhs], w=[out], rk=rk, wk=wk)

    def tr(self, out, in_, rk=None, wk=None):
        n = in_.shape[0]
        self.op("pe", lambda e: e.transpose(out=out, in_=in_, identity=self.ident[0:n, 0:n]),
                r=[in_, self.ident], w=[out], rk=rk, wk=wk)

    def evac(self, out, in_, rk=None, wk=None):
        self.copy("act" if self.rr() % 2 else "dve", out, in_, rk=rk, wk=wk)

    def bcast_row(self, name, src_row, n, dt=F32):
        t = self.sb(name, [128, n], dt)
        self.dma("sp", t[:], src_row.partition_broadcast(128))
        return t

    def load_w(self, dst, src, kc, ncols, stages):
        step = 512
        for c0 in range(0, ncols, step):
            c1 = min(ncols, c0 + step)
            stage = stages[self.rr() % len(stages)]
            self.dma("sp", stage[:, 0:kc, 0:c1 - c0], src[:, c0:c1].rearrange("(k p) n -> p k n", p=128))
            self.copy("dve" if self.rr() % 3 else "act", dst[:, 0:kc, c0:c1], stage[:, 0:kc, 0:c1 - c0])

    def rms_scale(self, src, nt, n, ssq, tmp):
        self.act(tmp[0:nt, 0:n], src, AF.Square, accum=ssq[0:nt, 0:1])
        self.act(ssq[0:nt, 0:1], ssq[0:nt, 0:1], AF.Sqrt, scale=1.0 / n, bias=EPS)
        self.recip(ssq[0:nt, 0:1], ssq[0:nt, 0:1])

    def transpose_cols(self, dst, src, nt, ncols, tb):
        kc = ncols // 128
        for k in range(kc):
            self.tr(tb[:, k * 128:k * 128 + nt], src[0:nt, k * 128:(k + 1) * 128])
        self.evac(dst[:, 0:kc, 0:nt], tb[:, 0:kc * 128].rearrange("p (k t) -> p k t", k=kc)[:, :, 0:nt])


def build_dense(layer):
    diff = (layer == 0)
    R = 1536 if diff else 1024
    lam_init = 0.8 - 0.6 * math.exp(-0.3 * layer)
    nc = bass.Bass("TRN2", target_bir_lowering=False)
    with ExitStack() as es:
        kb = KB(nc, es)
        x_in = kb.din("x_in", [TB, D])
        mixT = kb.din("mixT", [R, TB])
        w_out = kb.din("w_out", [D, D])
        w_ff1 = kb.din("w_ff1", [D, 4 * D])
        w_ff2 = kb.din("w_ff2", [4 * D, D])
        g_ff = kb.din("g_ff", [D])
        g_next = kb.din("g_next", [D])
        if diff:
            lamv = kb.din("lamv", [256])
            subln = kb.din("subln", [128])
        x_out = kb.dout("x_out", [TB, D])
        hn_out = kb.dout("hn_out", [TB, D])
        scr_w1 = kb.dscr("scr_w1", [4, 128, 8192], BF16)
        scr_w2 = kb.dscr("scr_w2", [4, 128, 8192], BF16)

        stage = [kb.sb("stage%d" % i, [128, 8, 512], F32) for i in range(2)]
        wout_b = kb.sb("wout_b", [128, 8, D], BF16)
        w1g = kb.sb("w1g", [128, 8, 1024], BF16)
        w2g = kb.sb("w2g", [128, 8, 1024], BF16)
        gff_bc = kb.bcast_row("gff_bc", g_ff, D)
        gnx_bc = kb.bcast_row("gnx_bc", g_next, D)
        kb.load_w(wout_b, w_out, 8, D, stage)
        x1 = kb.sb("x1", [128, 4, D], F32)
        hb = kb.sb("hb", [128, D], BF16)
        hT = kb.sb("hT", [128, 8, 512], BF16)
        oTb = kb.sb("oTb", [128, 8, 512], BF16)
        mst = [kb.sb("mst%d" % i, [128, 512], F32) for i in range(3)]
        aT = kb.sb("aT", [128, 8, 512], BF16)
        rl = [kb.sb("rl%d" % i, [128, 512], F32) for i in range(2)]
        ssq = kb.sb("ssq", [128, 1], F32)
        junk = kb.sb("junk", [128, D], F32)
        hno = kb.sb("hno", [128, D], F32)
        if diff:
            lv = kb.bcast_row("lv", lamv, 256)
            sub_col = kb.sb("sub_col", [128, 1], F32)
            kb.dma("sp", sub_col[:], subln.rearrange("(p o) -> p o", o=1))
            lam = kb.sb("lam", [128, 1], F32)
            l2 = kb.sb("l2", [128, 2], F32)
            lj = kb.sb("lj", [128, 64], F32)
            for j in range(2):
                kb.tt("dve", lj[:], lv[:, 128 * j:128 * j + 64], lv[:, 128 * j + 64:128 * j + 128], ALU.mult)
                kb.op("dve", lambda e, j=j: e.reduce_sum(out=l2[:, j:j + 1], in_=lj[:], axis=mybir.AxisListType.X),
                      r=[lj], w=[l2])
            kb.act(l2[:], l2[:], AF.Exp)
            kb.tt("dve", lam[:], l2[:, 1:2], l2[:, 0:1], ALU.subtract)
            kb.ts("dve", lam[:], lam[:], -lam_init, None, ALU.add)
            dtmp = kb.sb("dtmp", [128, 512], F32)
            dsq = kb.sb("dsq", [128, 512], F32)
            drs = kb.sb("drs", [128, 512], F32)

        blocks = [(i * 512, 512) for i in range(4)] + [(TOK, DSEQ)]
        for (t0, ntb) in blocks:
            ntile = max(1, ntb // 128)
            tl = min(ntb, 128)
            if diff:
                for k in range(4):
                    m = mst[k % 3]
                    kb.dma("sp", m[:, 0:ntb], mixT[k * 128:(k + 1) * 128, t0:t0 + ntb])
                    kb.copy("pool", oTb[:, k, 0:ntb], m[:, 0:ntb])
                for hd in range(4):
                    n1, n2 = mst[0], mst[1]
                    kb.dma("sp", n1[:, 0:ntb], mixT[512 + hd * 128:512 + (hd + 1) * 128, t0:t0 + ntb])
                    kb.dma("sp", n2[:, 0:ntb], mixT[1024 + hd * 128:1024 + (hd + 1) * 128, t0:t0 + ntb])
                    kb.stt(dtmp[:, 0:ntb], n2[:, 0:ntb], lam[:, 0:1], n1[:, 0:ntb], ALU.mult, ALU.add)
                    kb.act(dsq[:, 0:ntb], dtmp[:, 0:ntb], AF.Square)
                    kb.mm(kb.bank[4][:, 0:ntb], kb.ones32[:, :], dsq[:, 0:ntb])
                    kb.act(drs[:, 0:ntb], kb.bank[4][:, 0:ntb], AF.Sqrt, scale=1.0 / 128, bias=EPS)
                    kb.recip(drs[:, 0:ntb], drs[:, 0:ntb])
                    kb.tt("dve", dtmp[:, 0:ntb], dtmp[:, 0:ntb], drs[:, 0:ntb], ALU.mult)
                    kb.ts("dve", oTb[:, 4 + hd, 0:ntb], dtmp[:, 0:ntb], sub_col[:, 0:1], 1.0 - lam_init,
                          ALU.mult, ALU.mult)
            else:
                for k in range(8):
                    m = mst[k % 3]
                    kb.dma("sp", m[:, 0:ntb], mixT[k * 128:(k + 1) * 128, t0:t0 + ntb])
                    kb.copy("pool" if k % 2 else "dve", oTb[:, k, 0:ntb], m[:, 0:ntb])
            for ti in range(ntile):
                kb.dma("sp", x1[0:tl, ti, :], x_in[t0 + ti * 128:t0 + ti * 128 + tl, :])
                for half in range(2):
                    bk = kb.bank[half]
                    for k in range(8):
                        kb.mm(bk[0:tl, :], oTb[:, k, ti * 128:ti * 128 + tl], wout_b[:, k, half * 512:(half + 1) * 512],
                              start=(k == 0), stop=(k == 7))
                    kb.tt("dve", x1[0:tl, ti, half * 512:(half + 1) * 512], x1[0:tl, ti, half * 512:(half + 1) * 512],
                          bk[0:tl, :], ALU.add)
                kb.rms_scale(x1[0:tl, ti, :], tl, D, ssq, junk)
                kb.stt(hb[0:tl, :], x1[0:tl, ti, :], ssq[0:tl, 0:1], gff_bc[0:tl, :], ALU.mult, ALU.mult)
                tb = kb.tbank[ti % 2]
                for k in range(8):
                    kb.tr(tb[:, k * 128:k * 128 + tl], hb[0:tl, k * 128:(k + 1) * 128])
                kb.evac(hT[:, :, ti * 128:ti * 128 + tl],
                        tb[:, :].rearrange("p (k t) -> p k t", k=8)[:, :, 0:tl])
            for g in range(4):
                if t0 == 0:
                    kb.load_w(w1g, w_ff1[:, g * 1024:(g + 1) * 1024], 8, 1024, stage)
                    kb.load_w(w2g, w_ff2[g * 1024:(g + 1) * 1024, :], 8, 1024, stage)
                    kb.dma("pool", scr_w1[g], w1g[:, :, :].rearrange("p k n -> p (k n)"), wk=["scr_w1.%d" % g])
                    kb.dma("pool", scr_w2[g], w2g[:, :, :].rearrange("p k n -> p (k n)"), wk=["scr_w2.%d" % g])
                else:
                    kb.dma("sp", w1g[:, :, :].rearrange("p k n -> p (k n)"), scr_w1[g], rk=["scr_w1.%d" % g])
                    kb.dma("sp", w2g[:, :, :].rearrange("p k n -> p (k n)"), scr_w2[g], rk=["scr_w2.%d" % g])
                for fc in range(8):
                    bk = kb.bank[2 + fc % 2]
                    for k in range(8):
                        kb.mm(bk[:, 0:ntb], w1g[:, k, fc * 128:(fc + 1) * 128], hT[:, k, 0:ntb],
                              start=(k == 0), stop=(k == 7))
                    r_ = rl[fc % 2]
                    kb.act(r_[:, 0:ntb], bk[:, 0:ntb], AF.Relu)
                    kb.tt("pool" if fc % 2 else "dve", aT[:, fc, 0:ntb], r_[:, 0:ntb], r_[:, 0:ntb], ALU.mult)
                for ti in range(ntile):
                    for half in range(2):
                        bk = kb.bank[half]
                        for fc in range(8):
                            kb.mm(bk[0:tl, :], aT[:, fc, ti * 128:ti * 128 + tl], w2g[:, fc, half * 512:(half + 1) * 512],
                                  start=(fc == 0), stop=(fc == 7))
                        kb.tt("dve", x1[0:tl, ti, half * 512:(half + 1) * 512],
                              x1[0:tl, ti, half * 512:(half + 1) * 512], bk[0:tl, :], ALU.add)
            for ti in range(ntile):
                kb.dma("pool", x_out[t0 + ti * 128:t0 + ti * 128 + tl, :], x1[0:tl, ti, :])
                kb.rms_scale(x1[0:tl, ti, :], tl, D, ssq, junk)
                kb.stt(hno[0:tl, :], x1[0:tl, ti, :], ssq[0:tl, 0:1], gnx_bc[0:tl, :], ALU.mult, ALU.mult)
                kb.dma("pool", hn_out[t0 + ti * 128:t0 + ti * 128 + tl, :], hno[0:tl, :])
        kb.P.finish()
    return nc


class Attn:
    def __init__(self, kb, sb=True):
        self.kb = kb
        self.Pt = [kb.sb("Pt%d" % i, [128, 512], BF16) for i in range(3)]
        self.n = 0
        if sb:
            self.Us = [kb.sb("Us%d" % i, [128, 512], F32) for i in range(3)]
            self.Ls = [kb.sb("Ls%d" % i, [128, 512], BF16) for i in range(3)]
            self.Es = [kb.sb("Es%d" % i, [128, 512], F32) for i in range(3)]
        self.osb = kb.sb("osb", [128, 512], F32)
        self.rs = kb.sb("rs", [128, 512], F32)
        self.onrm = [kb.sb("onrm%d" % i, [128, 512], F32) for i in range(2)]
        self.hk = kb.sb("hk", [128, 512], F32)
        if not sb:
            return
        self.utri = kb.sb("utri", [128, 128], BF16)
        self.lst = kb.sb("lst", [128, 128], BF16)
        kb.op("pool", lambda e: e.affine_select(out=self.utri[:], in_=kb.onesb[:, 0:128], pattern=[[-1, 128]],
                                                compare_op=ALU.is_ge, fill=0.0, base=0, channel_multiplier=1),
              r=[kb.onesb], w=[self.utri])
        kb.op("pool", lambda e: e.affine_select(out=self.lst[:], in_=kb.onesb[:, 0:128], pattern=[[1, 128]],
                                                compare_op=ALU.is_gt, fill=0.0, base=0, channel_multiplier=-1),
              r=[kb.onesb], w=[self.lst])
        self.n = 0

    def build_E(self, dst, g, base, nk, W, mask=None):
        kb = self.kb
        src = bass.AP(tensor=g.tensor, offset=g.offset + base, ap=[[1, nk], [1, W]])
        kb.dma("sp", self.hk[0:nk, 0:W], src, rk=[_key(g)])
        rev = self.hk[0:nk, 0:W][:, ::-1]
        if mask is None:
            kb.copy("dve", dst, rev)
        else:
            kb.tt("dve", dst, rev, mask, ALU.mult)

    def run(self, kind, nq, W, units, KT, QT, V, scale, out_dma, far_bias=None, dv=64,
            kkeys=None, qkeys=None, vkeys=None):
        kb = self.kb
        B = kb.bank
        flat = []
        for qb in range(nq):
            ul = units(qb)
            for ui, (kt, nk, E, far) in enumerate(ul):
                self.n += 1
                flat.append(dict(qb=qb, first=(ui == 0), last=(ui == len(ul) - 1), kt=kt, nk=nk, E=E, far=far,
                                 sl=self.n % 3, n=self.n))
        sbank = [B[0], B[1], B[4] if kind == "sb" else B[5]]

        def stage_a(d):
            sl, nk = d["sl"], d["nk"]
            q_ap = QT(d["qb"])
            qk = qkeys(d["qb"]) if qkeys else [_key(q_ap)]
            k_ap = KT(d["kt"])
            kk = kkeys(d["kt"]) if kkeys else [_key(k_ap)]
            S = sbank[sl][0:nk, 0:W]
            kb.mm(S, k_ap, q_ap, rk=kk + qk, wk=[_key(sbank[sl])])
            if kind == "softmax":
                Pt = self.Pt[sl][0:nk, 0:W]
                bias = far_bias[0:nk, 0:1] if (d["far"] and far_bias is not None) else 0.0
                kb.act(Pt, S, AF.Exp, scale=scale, bias=bias)
                if d["E"] is not None:
                    kb.tt("pool" if d["n"] % 4 < 2 else "dve", Pt, Pt, d["E"], ALU.mult)
            else:
                U = self.Us[sl][0:nk, 0:W]
                L = self.Ls[sl][0:nk, 0:W]
                kb.act(U, S, AF.Exp, scale=scale)
                kb.act(L, U, AF.Ln, scale=1.0, bias=1.0)
                if d["E"] is not None:
                    kb.tt("pool", L, L, d["E"], ALU.mult)

        def sb_head(d):
            sl, nk = d["sl"], d["nk"]
            L = self.Ls[sl][0:nk, 0:W]
            kb.mm(B[5][:, 0:W], self.utri[0:nk, :], L, start=d["first"], stop=True, skip=True)
            kb.act(self.Es[sl][0:nk, 0:W], B[5][0:nk, 0:W], AF.Exp, scale=-1.0)

        def stage_b1(d):
            sl, nk = d["sl"], d["nk"]
            Pt = self.Pt[sl][0:nk, 0:W]
            kb.tt("dve", Pt, self.Us[sl][0:nk, 0:W], self.Es[sl][0:nk, 0:W], ALU.mult)
            if d["E"] is not None:
                kb.tt("pool", Pt, Pt, d["E"], ALU.mult)
            if not d["last"]:
                kb.mm(B[5][:, 0:W], self.lst[0:nk, :], self.Ls[sl][0:nk, 0:W], start=False, stop=True, skip=True)

        def stage_b(d):
            sl, nk, qb = d["sl"], d["nk"], d["qb"]
            Pt = self.Pt[sl][0:nk, 0:W]
            v_aps = V(d["kt"])
            nV = len(v_aps)
            vk = vkeys(d["kt"]) if vkeys else [_key(v_aps[0])]
            for vi, v_ap in enumerate(v_aps):
                ob = B[2 + (qb * nV + vi) % 2]
                rows = v_ap.shape[1]
                kb.mm(ob[0:rows, 0:W], v_ap, Pt, start=d["first"], stop=d["last"], rk=vk + [_key(Pt)], wk=[_key(ob)])
            nfill = 1 if (FILLK and kind == "softmax" and d["n"] % FILLK == 0) else 0
            if nfill and W == 512:
                fb_ = kb.tbank[0][:, :].bitcast(F32)[:, 0:FILLN]
                for _ in range(nfill):
                    kb.op("pe", lambda e: e.matmul(fb_, kb.onesb[:, 0:128], self.Pt[sl][:, 0:FILLN], start=True, stop=True),
                          rk=[], wk=[])
            if not d["last"]:
                return
            for vi in range(nV):
                ob = B[2 + (qb * nV + vi) % 2]
                on = self.onrm[(qb * nV + vi) % 2]
                if kind == "sb":
                    kb.copy("act", on[0:dv, 0:W], ob[0:dv, 0:W])
                else:
                    kb.copy("act", self.osb[0:dv + 1, 0:W], ob[0:dv + 1, 0:W])
                    kb.recip(self.rs[dv:dv + 1, 0:W], self.osb[dv:dv + 1, 0:W])
                    kb.mm(B[4][0:dv, 0:W], kb.ones32[dv:dv + 1, 0:dv], self.rs[dv:dv + 1, 0:W])
                    kb.tt("dve", on[0:dv, 0:W], self.osb[0:dv, 0:W], B[4][0:dv, 0:W], ALU.mult)
                out_dma(qb, vi, on[0:dv, 0:W])

        if not flat:
            return
        for d in flat[0:2]:
            stage_a(d)
        if kind == "sb":
            sb_head(flat[0])
        for i, d in enumerate(flat):
            if i + 2 < len(flat):
                stage_a(flat[i + 2])
            if kind == "sb":
                stage_b1(d)
                if i + 1 < len(flat):
                    sb_head(flat[i + 1])
            stage_b(d)


def memset_rect(kb, tile, p0, p1, c0, c1, val):
    if p1 > p0 and c1 > c0:
        kb.op("pool", lambda e: e.memset(tile[p0:p1, c0:c1], val), w=[tile])


CA_G = 1536
CA_OFF = 1023


def ca_onehot():
    rel = np.arange(CA_G) - CA_OFF
    idx = np.clip(rel, -128, 128) + 128
    oh = np.zeros((257, CA_G), np.float32)
    oh[idx, np.arange(CA_G)] = 1.0
    return oh


def gather_exp_vector(kb, oh_dram, nrows, tab_dram, ncol, n, g_scr):
    kchunks = [(k0, min(128, nrows - k0)) for k0 in range(0, nrows, 128)]
    oh_sb = kb.sb("oh_sb", [128, len(kchunks), 512], F32)
    tab_sb = kb.sb("tab_sb", [128, len(kchunks), ncol], F32)
    gs = kb.sb("g_sb", [16, 512], F32)
    for ki, (k0, kn) in enumerate(kchunks):
        kb.dma("sp", tab_sb[0:kn, ki, :], tab_dram[k0:k0 + kn, :])
    for c0 in range(0, n, 512):
        cw = min(512, n - c0)
        for ki, (k0, kn) in enumerate(kchunks):
            kb.dma("sp", oh_sb[0:kn, ki, 0:cw], oh_dram[k0:k0 + kn, c0:c0 + cw])
        for ki, (k0, kn) in enumerate(kchunks):
            kb.mm(kb.bank[4][0:ncol, 0:cw], tab_sb[0:kn, ki, :], oh_sb[0:kn, ki, 0:cw],
                  start=(ki == 0), stop=(ki == len(kchunks) - 1))
        kb.act(gs[0:ncol, 0:cw], kb.bank[4][0:ncol, 0:cw], AF.Exp)
        kb.dma("sp", g_scr[0:ncol, c0:c0 + cw], gs[0:ncol, 0:cw])


import os
STOP = int(os.environ.get('A_STOP', '99'))
FILLK = int(os.environ.get('FILLK', '1'))
FILLN = int(os.environ.get('FILLN', '512'))


def build_attn1():
    nc = bass.Bass("TRN2", target_bir_lowering=False)
    with ExitStack() as es:
        kb = KB(nc, es)
        hp = kb.din("hp", [SEQ, D])
        hs = kb.din("hs", [DSEQ, D])
        w_p = kb.din("w_p", [D, 384])
        w_s = kb.din("w_s", [D, 3072])
        c_sbk = kb.din("c_sbk", [PAST, 512])
        c_sbv = kb.din("c_sbv", [PAST, 512])
        c_cak = kb.din("c_cak", [512, 512])
        c_cav = kb.din("c_cav", [512, 512])
        ca_oh = kb.din("ca_oh", [257, CA_G])
        ca_tab = kb.din("ca_tab", [257, 9])
        o_sk = kb.dout("o_sk", [SEQ, 64])
        o_sv = kb.dout("o_sv", [SEQ, 64])
        o_ck = kb.dout("o_ck", [512, 64])
        o_cv = kb.dout("o_cv", [512, 64])
        oT_sb = kb.dout("oT_sb", [64, SEQ])
        oT_ca = kb.dout("oT_ca", [64, SEQ])
        s_sk = kb.dout("s_sk", [DSEQ, 512])
        s_sv = kb.dout("s_sv", [DSEQ, 512])
        s_cak = kb.dout("s_cak", [512, 512])
        s_cav = kb.dout("s_cav", [512, 512])
        soT = kb.dout("soT", [1024, DSEQ])
        g_scr = kb.dscr("g_scr", [9, CA_G])
        scr_cqk = kb.dscr("scr_cqk", [64, 2, SEQ], BF16)
        scr_cv = kb.dscr("scr_cv", [SEQ, 64], BF16)

        at = Attn(kb)
        if STOP == -1:
            kb.dma("sp", o_ck[0:128, :], at.utri[:, 0:64].bitcast(F32)[:, 0:32] if False else kb.ones32[:, 0:64])
            kb.P.finish()
            return nc
        gather_exp_vector(kb, ca_oh, 257, ca_tab, 9, CA_G, g_scr)
        if STOP == -2:
            kb.P.finish()
            return nc
        stage = kb.sb("stage", [128, 8, 256], F32)
        BQ = kb.sb("BQ", [128, SEQ], BF16)
        BK = kb.sb("BK", [128, SEQ], BF16)
        BV = kb.sb("BV", [128, NT * 65], BF16)
        sc = 64 ** -0.5
        if STOP == 0:
            kb.P.finish()
            return nc

        xts = [kb.sb("xt%d" % i, [128, D], F32) for i in range(2)]
        hbs = [kb.sb("hb%d" % i, [128, D], BF16) for i in range(2)]
        xt, hb = xts[0], hbs[0]
        hT = [kb.sb("hT%d" % i, [128, 8, 128], BF16) for i in range(2)]
        sp32 = [kb.sb("sp32_%d" % i, [128, 512], F32) for i in range(2)]
        cst = [kb.sb("cst%d" % i, [128, 512], F32) for i in range(2)]
        cb = [kb.sb("cb%d" % i, [128, 512], BF16) for i in range(2)]
        QTs = kb.sb("QTs", [128, 8, DSEQ], BF16)
        Esm = kb.sb("Esm", [128, 5, 8, DSEQ], BF16)
        Msm = kb.sb("Msm", [128, DSEQ], BF16)
        KTs = BQ[:, 0:4 * 2080].rearrange("p (a n) -> p a n", a=4)
        Vs = BK[:, 0:17 * 8 * 65].rearrange("p (t h d) -> p t h d", t=17, h=8)
        spb = BV[:, 0:3072]
        wsg = BV[:, 4096:8192].rearrange("p (k n) -> p k n", k=8)
        KTc = BK[:, 9000:9000 + 4 * 544].rearrange("p (a n) -> p a n", a=4)
        Vc = BK[:, 11200:11200 + 5 * 8 * 65].rearrange("p (t h d) -> p t h d", t=5, h=8)

        kb.dma("sp", xt[0:DSEQ, :], hs)
        kb.copy("dve", hb[0:DSEQ, :], xt[0:DSEQ, :])
        kb.transpose_cols(hT[0], hb, DSEQ, D, kb.tbank[0])
        if STOP == 10:
            kb.P.finish()
            return nc
        for g in range(6):
            for c0 in range(0, 512, 256):
                kb.dma("sp", stage[:, :, :], w_s[:, g * 512 + c0:g * 512 + c0 + 256].rearrange("(k p) n -> p k n", p=128))
                kb.copy("dve", wsg[:, :, c0:c0 + 256], stage[:, :, :])
            bk = kb.bank[g % 2]
            DBG = os.environ.get("DBG", "")
            if "nomm" in DBG:
                if STOP == 11 and g == 1:
                    kb.P.finish()
                    return nc
                continue
            for k in range(8):
                kb.mm(bk[0:DSEQ, :], hT[0][:, k, 0:DSEQ], wsg[:, k, :], start=(k == 0), stop=(k == 7))
            if "nospb" not in DBG:
                kb.copy("dve", spb[0:DSEQ, g * 512:(g + 1) * 512], bk[0:DSEQ, :])
            if g in (1, 2, 4, 5) and "nost" not in DBG:
                s32 = sp32[g % 2]
                kb.copy("dve" if "dvecp" in DBG else "act", s32[0:DSEQ, :], bk[0:DSEQ, :],
                        rk=[_key(bk), "BV"] if "ser" in DBG else None)
                dst = {1: s_sk, 2: s_sv, 4: s_cak[480:512, :], 5: s_cav[480:512, :]}[g]
                if "nodma" not in DBG:
                    kb.dma("pool", dst, s32[0:DSEQ, :])
            if STOP == 11 and g == 1:
                kb.P.finish()
                return nc
        if STOP == 12:
            kb.P.finish()
            return nc
        for src_c, dst_c in ((c_cak, s_cak), (c_cav, s_cav)):
            stv = stage[0:120, :, :].rearrange("p k n -> p (k n)")
            kb.dma("sp", stv, src_c[32:512, :].rearrange("(a b) n -> a (b n)", a=120))
            kb.dma("pool", dst_c[0:480, :].rearrange("(a b) n -> a (b n)", a=120), stv)
        if STOP == 1:
            kb.P.finish()
            return nc
        for a in range(8):
            src0 = (0 if a < 4 else 1536) + (a % 4) * 128
            kb.tr(kb.tbank[1][:, a * DSEQ:(a + 1) * DSEQ], spb[0:DSEQ, src0:src0 + 128])
        kb.evac(QTs[:, :, :], kb.tbank[1][:, 0:8 * DSEQ].rearrange("p (a t) -> p a t", a=8))
        kb.op("pool", lambda e: e.memset(Vs[:, :, :, 64:65], 1.0), w=[BK])
        kb.op("pool", lambda e: e.memset(Vc[:, :, :, 64:65], 1.0), w=[BK])

        def fill_kv(ck_dram, cv_dram, ntile, KTv, Vv, kcol, vcol):
            for j in range(ntile):
                s0, s1 = cst[0], cst[1]
                kb.dma("sp", s0[:, :], ck_dram[j * 128:(j + 1) * 128, :])
                kb.dma("sp", s1[:, :], cv_dram[j * 128:(j + 1) * 128, :])
                kb.copy("pool", cb[0][:, :], s0[:, :])
                tb = kb.tbank[j % 2]
                for a in range(4):
                    kb.tr(tb[:, a * 128:(a + 1) * 128], cb[0][:, a * 128:(a + 1) * 128])
                kb.evac(KTv[:, :, j * 128:(j + 1) * 128], tb[:, 0:512].rearrange("p (a t) -> p a t", a=4))
                kb.copy("dve", Vv[:, j, :, 0:64], s1[:, :].rearrange("p (h d) -> p h d", h=8))
            tb = kb.tbank[ntile % 2]
            for a in range(4):
                kb.tr(tb[:, a * DSEQ:(a + 1) * DSEQ], spb[0:DSEQ, kcol + a * 128:kcol + (a + 1) * 128])
            kb.evac(KTv[:, :, ntile * 128:ntile * 128 + DSEQ], tb[:, 0:4 * DSEQ].rearrange("p (a t) -> p a t", a=4))
            kb.copy("dve", Vv[0:DSEQ, ntile, :, 0:64], spb[0:DSEQ, vcol:vcol + 512].rearrange("p (h d) -> p h d", h=8))

        fill_kv(c_sbk, c_sbv, 16, KTs, Vs, 512, 1024)
        fill_kv(c_cak, c_cav, 4, KTc, Vc, 2048, 2560)
        if STOP == 2:
            kb.P.finish()
            return nc
        kb.op("pool", lambda e: e.affine_select(out=Msm[0:DSEQ, :], in_=kb.onesb[0:DSEQ, 0:DSEQ], pattern=[[1, DSEQ]],
                                                compare_op=ALU.is_gt, fill=0.0, base=0, channel_multiplier=-1),
              r=[kb.onesb], w=[Msm])
        for h in range(8):
            for j in range(5):
                nk = 128 if j < 4 else DSEQ
                base = (128 * j + 480) if j < 4 else 992
                at.build_E(Esm[0:nk, j, h, :], g_scr[h, :], base, nk, DSEQ)
        if STOP == 3:
            kb.P.finish()
            return nc
        for h in range(8):
            pr, pb_ = h // 2, 64 * (h % 2)
            at.run("sb", 1, DSEQ,
                   lambda qb: [(16, DSEQ, Msm[0:DSEQ, :], False)] + [(kt, 128, None, False) for kt in range(15, -1, -1)],
                   lambda kt, pr=pr, pb_=pb_: KTs[pb_:pb_ + 64, pr, kt * 128:kt * 128 + (128 if kt < 16 else DSEQ)],
                   lambda qb, pr=pr, pb_=pb_: QTs[pb_:pb_ + 64, pr, :],
                   lambda kt, h=h: [Vs[0:(128 if kt < 16 else DSEQ), kt, h, 0:64]],
                   sc, lambda qb, vi, on, h=h: kb.dma("pool", soT[h * 64:(h + 1) * 64, :], on))
            at.run("softmax", 1, DSEQ,
                   lambda qb, h=h: [(j, 128 if j < 4 else DSEQ, Esm[0:(128 if j < 4 else DSEQ), j, h, :], False)
                                    for j in range(5)],
                   lambda kt, pr=pr, pb_=pb_: KTc[pb_:pb_ + 64, pr, kt * 128:kt * 128 + (128 if kt < 4 else DSEQ)],
                   lambda qb, pr=pr, pb_=pb_: QTs[pb_:pb_ + 64, 4 + pr, :],
                   lambda kt, h=h: [Vc[0:(128 if kt < 4 else DSEQ), kt, h, 0:65]],
                   sc, lambda qb, vi, on, h=h: kb.dma("pool", soT[512 + h * 64:512 + (h + 1) * 64, :], on))
        kb.P.barrier()
        if STOP == 4:
            kb.P.finish()
            return nc

        wp_b = kb.sb("wp_b", [128, 8, 384], BF16)
        kb.load_w(wp_b, w_p, 8, 384, stage) if False else None
        for c0 in range(0, 384, 256):
            c1 = min(384, c0 + 256)
            kb.dma("sp", stage[:, :, 0:c1 - c0], w_p[:, c0:c1].rearrange("(k p) n -> p k n", p=128))
            kb.copy("dve", wp_b[:, :, c0:c1], stage[:, :, 0:c1 - c0])
        po = [kb.sb("po%d" % i, [128, 384], F32) for i in range(2)]
        pb = [kb.sb("pb%d" % i, [128, 384], BF16) for i in range(2)]
        caT = [kb.sb("caT%d" % i, [64, 2, 128], BF16) for i in range(2)]
        BVv = BV[:, :].rearrange("p (t d) -> p t d", d=65)
        kb.op("pool", lambda e: e.memset(BVv[:, :, 64:65], 1.0), w=[BV])
        kb.P.barrier()
        for t in range(NT):
            sl = t % 2
            r0 = t * 128
            kb.dma("sp", xts[sl][:, :], hp[r0:r0 + 128, :])
            kb.copy("dve", hbs[sl][:, :], xts[sl][:, :])
            kb.transpose_cols(hT[sl], hbs[sl], 128, D, kb.tbank[sl])
            bk = kb.bank[sl]
            for k in range(8):
                kb.mm(bk[:, 0:384], hT[sl][:, k, :], wp_b[:, k, :], start=(k == 0), stop=(k == 7))
            kb.copy("act", po[sl][:, :], bk[:, 0:384])
            kb.copy("dve", pb[sl][:, :], bk[:, 0:384])
            kb.dma("pool", o_sk[r0:r0 + 128, :], po[sl][:, 64:128])
            kb.dma("pool", o_sv[r0:r0 + 128, :], po[sl][:, 128:192])
            if t >= NT - 4:
                rr0 = (t - (NT - 4)) * 128
                kb.dma("pool", o_ck[rr0:rr0 + 128, :], po[sl][:, 256:320])
                kb.dma("pool", o_cv[rr0:rr0 + 128, :], po[sl][:, 320:384])
            tb = kb.tbank[1 - sl]
            for j, c0 in enumerate((0, 64, 192, 256)):
                kb.tr(tb[0:64, j * 128:(j + 1) * 128], pb[sl][:, c0:c0 + 64])
            kb.copy("act", BQ[0:64, r0:r0 + 128], tb[0:64, 0:128], wk=["BQ.%d" % t])
            kb.copy("dve", BK[0:64, r0:r0 + 128], tb[0:64, 128:256], wk=["BK.%d" % t])
            kb.copy("act", caT[sl][:, :, :], tb[0:64, 256:512].rearrange("p (a t) -> p a t", a=2))
            kb.dma("pool", scr_cqk[:, :, r0:r0 + 128], caT[sl][:, :, :])
            kb.copy("act", BVv[:, t, 0:64], pb[sl][:, 128:192], wk=["BV.%d" % t])
            kb.dma("pool", scr_cv[r0:r0 + 128, :], pb[sl][:, 320:384])
        if STOP == 5:
            kb.P.finish()
            return nc

        Msb = kb.sb("Msb", [128, 4, 512], BF16)
        for i in range(4):
            kb.op("pool", lambda e, i=i: e.affine_select(out=Msb[:, i, :], in_=kb.onesb[:, 0:512], pattern=[[1, 512]],
                                                         compare_op=ALU.is_gt, fill=0.0, base=-128 * i,
                                                         channel_multiplier=-1), r=[kb.onesb], w=[Msb])
        at.run("sb", SEQ // 512, 512,
               lambda qb: [(kt, 128, (Msb[:, kt - 4 * qb, :] if kt >= 4 * qb else None), False)
                           for kt in range(4 * qb + 3, -1, -1)],
               lambda kt: BK[0:64, kt * 128:(kt + 1) * 128],
               lambda qb: BQ[0:64, qb * 512:(qb + 1) * 512],
               lambda kt: [BVv[:, kt, 0:64]],
               sc, lambda qb, vi, on: kb.dma("pool", oT_sb[:, qb * 512:(qb + 1) * 512], on),
               kkeys=lambda kt: ["BK.%d" % kt], qkeys=lambda qb: ["BQ.%d" % (4 * qb + i) for i in range(4)],
               vkeys=lambda kt: ["BV.%d" % kt])
        kb.P.barrier()

        kb.dma("sp", BQ[0:64, :], scr_cqk[:, 0, :], wk=["BQc"])
        kb.dma("sp", BK[0:64, :], scr_cqk[:, 1, :], wk=["BKc"])
        for c in range(16):
            kb.dma("sp", BVv[:, c * 8:(c + 1) * 8, 0:64],
                   scr_cv[c * 1024:(c + 1) * 1024, :].rearrange("(t p) d -> p t d", p=128), sem="BVc", wk=["BVc"])
        Eca = kb.sb("Eca", [128, 8, 512], BF16)
        Mt = kb.sb("Mt", [128, 512], BF16)
        for i in range(-4, 4):
            kb.op("pool", lambda e: e.memset(Mt[:, :], 0.0), w=[Mt])
            for a in range(2):
                kc = 2 * i + a
                memset_rect(kb, Mt, 64 * a, 64 * a + 64, 64 * max(0, kc), 64 * min(8, kc + 9), 1.0)
            at.build_E(Eca[:, i + 4, :], g_scr[8, :], 128 * i + 512, 128, 512, mask=Mt[:, :])
        at.run("softmax", SEQ // 512, 512,
               lambda qb: [(4 * qb + i, 128, Eca[:, i + 4, :], False) for i in range(-4, 4) if 4 * qb + i >= 0],
               lambda kt: BK[0:64, kt * 128:(kt + 1) * 128],
               lambda qb: BQ[0:64, qb * 512:(qb + 1) * 512],
               lambda kt: [BVv[:, kt, 0:65]],
               sc, lambda qb, vi, on: kb.dma("pool", oT_ca[:, qb * 512:(qb + 1) * 512], on),
               kkeys=lambda kt: ["BKc"], qkeys=lambda qb: ["BQc"], vkeys=lambda kt: ["BVc"])
        kb.P.finish()
    return nc


T5_G = 1152
T5_OFF = 639


def t5_onehot():
    import jax
    import jax.numpy as jnp
    with jax.default_device(jax.devices("cpu")[0]):
        rel = jnp.arange(T5_G, dtype=jnp.int32) - T5_OFF
        nb, max_exact = 16, 8
        ret = jnp.where(rel > 0, nb, 0)
        n = jnp.abs(rel)
        nf = jnp.maximum(n, 1).astype(jnp.float32)
        large = max_exact + (jnp.log(nf / max_exact) / math.log(128 / max_exact) * (nb - max_exact)).astype(jnp.int32)
        large = jnp.minimum(large, nb - 1)
        b = np.asarray(ret + jnp.where(n < max_exact, n, large))
    oh = np.zeros((32, T5_G), np.float32)
    oh[b, np.arange(T5_G)] = 1.0
    return oh


def rope_table(pos):
    half = 16
    inv = (10000.0 ** (-np.arange(half, dtype=np.float32) / half)).astype(np.float32)
    ang = pos.astype(np.float32)[:, None] * inv[None, :]
    c, s = np.cos(ang).astype(np.float32), np.sin(ang).astype(np.float32)
    return np.ascontiguousarray(np.concatenate([c, c, -s, s], axis=1))


def build_attn0():
    nc = bass.Bass("TRN2", target_bir_lowering=False)
    with ExitStack() as es:
        kb = KB(nc, es)
        xp = kb.din("xp", [SEQ, D])
        xs = kb.din("xs", [DSEQ, D])
        g_mix = kb.din("g_mix", [D])
        q_norm = kb.din("q_norm", [256])
        kv_norm = kb.din("kv_norm", [128])
        w_lat = kb.din("w_lat", [D, 416])
        w_dp = kb.din("w_dp", [D, 256])
        w_dall = kb.din("w_dall", [D, 1536])
        w_uq = kb.din("w_uq", [256, 768])
        w_uk = kb.din("w_uk", [128, 512])
        w_uv = kb.din("w_uv", [128, 512])
        w_uqh = kb.din("w_uqh", [256, 96])
        w_ukvh = kb.din("w_ukvh", [128, 128])
        c_ckv = kb.din("c_ckv", [PAST, 128])
        c_kr = kb.din("c_kr", [PAST, 32])
        c_dk = kb.din("c_dk", [PAST, 512])
        c_dv = kb.din("c_dv", [PAST, 512])
        rope_p = kb.din("rope_p", [SEQ, 64])
        rope_s = kb.din("rope_s", [DSEQ, 64])
        t5_oh = kb.din("t5_oh", [32, T5_G])
        t5_tab = kb.din("t5_tab", [32, 5])
        o_lat = kb.dout("o_lat", [SEQ, 160])
        o_dk = kb.dout("o_dk", [SEQ, 64])
        o_dv = kb.dout("o_dv", [SEQ, 128])
        oT_m = kb.dout("oT_m", [64, SEQ])
        oT_d = kb.dout("oT_d", [128, SEQ])
        s_lat = kb.dout("s_lat", [DSEQ, 160])
        s_dk = kb.dout("s_dk", [DSEQ, 512])
        s_dv = kb.dout("s_dv", [DSEQ, 512])
        soT_m = kb.dout("soT_m", [512, DSEQ])
        soT_d = kb.dout("soT_d", [1024, DSEQ])
        g_scr = kb.dscr("g_scr", [5, T5_G])
        scr_dqk = kb.dscr("scr_dqk", [64, 2, SEQ], BF16)
        scr_dv = kb.dscr("scr_dv", [SEQ, 130], BF16)

        at = Attn(kb, sb=False)
        gather_exp_vector(kb, t5_oh, 32, t5_tab, 5, T5_G, g_scr)
        stage = kb.sb("stage", [128, 8, 128], F32)
        wlat_b = kb.sb("wlat_b", [128, 8, 416], BF16)
        wdp_b = kb.sb("wdp_b", [128, 8, 256], BF16)
        wuq_b = kb.sb("wuq_b", [128, 2, 768], BF16)
        wuk_b = kb.sb("wuk_b", [128, 1, 512], BF16)
        wuv_b = kb.sb("wuv_b", [128, 1, 512], BF16)
        wuqh_b = kb.sb("wuqh_b", [128, 2, 96], BF16)
        wukvh_b = kb.sb("wukvh_b", [128, 1, 128], BF16)
        for dst, src, kc, ncol in ((wlat_b, w_lat, 8, 416), (wdp_b, w_dp, 8, 256), (wuq_b, w_uq, 2, 768),
                                   (wuk_b, w_uk, 1, 512), (wuv_b, w_uv, 1, 512), (wuqh_b, w_uqh, 2, 96),
                                   (wukvh_b, w_ukvh, 1, 128)):
            for c0 in range(0, ncol, 128):
                c1 = min(ncol, c0 + 128)
                kb.dma("sp", stage[:, 0:kc, 0:c1 - c0], src[:, c0:c1].rearrange("(k p) n -> p k n", p=128))
                kb.copy("dve", dst[:, 0:kc, c0:c1], stage[:, 0:kc, 0:c1 - c0])
        gmix_bc = kb.bcast_row("gmix_bc", g_mix, D)
        qn_bc = kb.bcast_row("qn_bc", q_norm, 256)
        kvn_bc = kb.bcast_row("kvn_bc", kv_norm, 128)
        fb = kb.sb("fb", [128, 5], F32)
        kb.dma("sp", fb[:, :], t5_tab[15, :].partition_broadcast(128))
        xt = kb.sb("xt", [128, D], F32)
        hb = kb.sb("hb", [128, D], BF16)
        hT = [kb.sb("hT0", [128, 8, 128], BF16)] * 2
        cs = [kb.sb("cs%d" % i, [128, 64], F32) for i in range(2)]
        ssq = kb.sb("ssq", [128, 4], F32)
        latsb = kb.sb("latsb", [128, 416], F32)
        dpsb = kb.sb("dpsb", [128, 512], F32)
        lat_o = [kb.sb("lat_o%d" % i, [128, 160], F32) for i in range(2)]
        cqn = kb.sb("cqn", [128, 256], BF16)
        ckvb = kb.sb("ckvb", [128, 128], BF16)
        cqT = kb.sb("cqT", [128, 2, 128], BF16)
        ckvT = kb.sb("ckvT", [128, 1, 128], BF16)
        upsb = kb.sb("upsb", [128, 1024], F32)
        junk = upsb
        rA = kb.sb("rA", [128, 32], F32)
        rB = kb.sb("rB", [128, 32], F32)
        qf = kb.sb("qf", [128, 8, 96], BF16)
        kf = kb.sb("kf", [128, 8, 96], BF16)
        dqkb = kb.sb("dqkb", [128, 512], BF16)
        dvt = [kb.sb("dvt%d" % i, [128, 130], BF16) for i in range(2)]
        dT = [kb.sb("dT%d" % i, [64, 2, 128], BF16) for i in range(2)]
        krb = kb.sb("krb", [128, 32], BF16)
        cst = [kb.sb("cst%d" % i, [128, 512], F32) for i in range(2)]
        cb = kb.sb("cb", [128, 512], BF16)
        QTm = kb.sb("QTm", [128, 8, DSEQ], BF16)
        QTd = kb.sb("QTd", [128, 4, DSEQ], BF16)
        Esd = kb.sb("Esd", [128, 2, 4, DSEQ], BF16)
        Mcc = kb.sb("Mcc", [128, 4, 512], BF16)
        Ed = kb.sb("Ed", [128, 5, 512], BF16)
        BQ = kb.sb("BQ", [128, SEQ], BF16)
        BK = kb.sb("BK", [128, SEQ], BF16)
        BV = kb.sb("BV", [128, NT * 130], BF16)
        BVv = BV[:, :].rearrange("p (t d) -> p t d", d=130)
        for i in range(2):
            kb.op("pool", lambda e, i=i: e.memset(dvt[i][:, :], 1.0), w=[dvt[i]])
        kb.op("pool", lambda e: e.memset(Mcc[:, :, :], 1.0), w=[Mcc])
        for i in range(4):
            for a in range(2):
                kc = 2 * i + a
                if kc > 0:
                    kb.op("pool", lambda e, i=i, a=a, kc=kc: e.memset(Mcc[64 * a:64 * a + 64, i, 0:64 * kc], 0.0), w=[Mcc])
        for i in range(-1, 4):
            at.build_E(Ed[:, i + 1, :], g_scr[4, :], 128 * i + 128, 128, 512, mask=(Mcc[:, i, :] if i >= 0 else None))

        def rope(dst, src, tab, nt):
            kb.tt("dve", rA[0:nt, :], src, tab[0:nt, 0:32], ALU.mult)
            sw = src.rearrange("p (a b) -> p a b", a=2)[:, ::-1, :]
            kb.tt("dve", rB[0:nt, :].rearrange("p (a b) -> p a b", a=2), sw,
                  tab[0:nt, 32:64].rearrange("p (a b) -> p a b", a=2), ALU.mult)
            kb.tt("dve", dst, rA[0:nt, :], rB[0:nt, :], ALU.add)

        xt_all = [xt, stage[:, :, :].rearrange("p k n -> p (k n)")]
        hb_all = [hb, cst[0][:, :].bitcast(BF16)]
        hT_all = [hT[0], cst[1][:, :].bitcast(BF16).rearrange("p (k n) -> p k n", k=8)]

        def latent_tile(x_src, tab_src, nt, sl, lat_dst, dbl=False):
            xt = xt_all[sl if dbl else 0]
            hb = hb_all[sl if dbl else 0]
            hT = [hT_all[sl if dbl else 0]] * 2
            kb.dma("sp", xt[0:nt, :], x_src)
            kb.dma("sp", cs[sl][0:nt, :], tab_src)
            kb.rms_scale(xt[0:nt, :], nt, D, ssq[:, 0:1], junk)
            kb.stt(hb[0:nt, :], xt[0:nt, :], ssq[0:nt, 0:1], gmix_bc[0:nt, :], ALU.mult, ALU.mult)
            kb.transpose_cols(hT[sl], hb, nt, D, kb.tbank[0])
            for k in range(8):
                kb.mm(kb.bank[0][0:nt, 0:416], hT[sl][:, k, 0:nt], wlat_b[:, k, :], start=(k == 0), stop=(k == 7))
            kb.copy("act", latsb[0:nt, :], kb.bank[0][0:nt, 0:416])
            lo = lat_o[sl]
            kb.rms_scale(latsb[0:nt, 0:256], nt, 256, ssq[:, 1:2], junk)
            kb.stt(cqn[0:nt, :], latsb[0:nt, 0:256], ssq[0:nt, 1:2], qn_bc[0:nt, :], ALU.mult, ALU.mult)
            kb.rms_scale(latsb[0:nt, 256:384], nt, 128, ssq[:, 2:3], junk)
            kb.stt(lo[0:nt, 0:128], latsb[0:nt, 256:384], ssq[0:nt, 2:3], kvn_bc[0:nt, :], ALU.mult, ALU.mult)
            kb.copy("dve", ckvb[0:nt, :], lo[0:nt, 0:128])
            rope(lo[0:nt, 128:160], latsb[0:nt, 384:416], cs[sl], nt)
            kb.dma("pool", lat_dst, lo[0:nt, :])
            kb.transpose_cols(cqT, cqn, nt, 256, kb.tbank[1])
            kb.transpose_cols(ckvT, ckvb, nt, 128, kb.tbank[1])

        sc_m = 96 ** -0.5
        sc_d = 64 ** -0.5

        spb = BV[:, 0:1536]
        wsg = BV[:, 2048:6144].rearrange("p (k n) -> p k n", k=8)
        KTm = BQ[:, 0:4 * 2080].rearrange("p (a n) -> p a n", a=4)
        KTd = BV[:, 6400:6400 + 4 * 2080].rearrange("p (a n) -> p a n", a=4)
        Vm = BK[:, 0:17 * 4 * 65].rearrange("p (t h d) -> p t h d", t=17, h=4)
        Vd = BK[:, 4480:4480 + 17 * 4 * 130].rearrange("p (t h d) -> p t h d", t=17, h=4)
        latent_tile(xs, rope_s, DSEQ, 0, s_lat)
        los = lat_o[0]
        for g in range(3):
            for c0 in range(0, 512, 128):
                kb.dma("sp", stage[:, :, :], w_dall[:, g * 512 + c0:g * 512 + c0 + 128].rearrange("(k p) n -> p k n", p=128))
                kb.copy("dve", wsg[:, :, c0:c0 + 128], stage[:, :, :])
            bk = kb.bank[1]
            for k in range(8):
                kb.mm(bk[0:DSEQ, :], hT[0][:, k, 0:DSEQ], wsg[:, k, :], start=(k == 0), stop=(k == 7))
            kb.copy("dve", spb[0:DSEQ, g * 512:(g + 1) * 512], bk[0:DSEQ, :])
            if g > 0:
                kb.copy("act", dpsb[0:DSEQ, :], bk[0:DSEQ, :])
                kb.dma("pool", s_dk if g == 1 else s_dv, dpsb[0:DSEQ, :])
        for j0, w in ((0, 512), (512, 256)):
            for j in range(2):
                kb.mm(kb.bank[2][0:DSEQ, 0:w], cqT[:, j, 0:DSEQ], wuq_b[:, j, j0:j0 + w], start=(j == 0), stop=(j == 1))
            kb.copy("act", upsb[0:DSEQ, j0:j0 + w], kb.bank[2][0:DSEQ, 0:w])
        for h in range(8):
            kb.copy("pool", qf[0:DSEQ, h, 0:64], upsb[0:DSEQ, h * 96:h * 96 + 64])
            rope(qf[0:DSEQ, h, 64:96], upsb[0:DSEQ, h * 96 + 64:h * 96 + 96], cs[0], DSEQ)
        for h in range(8):
            kb.tr(kb.tbank[0][0:96, h * DSEQ:(h + 1) * DSEQ], qf[0:DSEQ, h, :])
        kb.evac(QTm[0:96, :, :], kb.tbank[0][0:96, 0:8 * DSEQ].rearrange("p (a t) -> p a t", a=8))
        for h in range(4):
            kb.tr(kb.tbank[1][:, h * DSEQ:(h + 1) * DSEQ], spb[0:DSEQ, h * 128:(h + 1) * 128])
        kb.evac(QTd[:, :, :], kb.tbank[1][:, 0:4 * DSEQ].rearrange("p (a t) -> p a t", a=4))
        kb.op("pool", lambda e: e.memset(Vd[:, :, :, :], 1.0), w=[BK])
        for j in range(17):
            nk = 128 if j < 16 else DSEQ
            if j < 16:
                kb.dma("sp", cst[0][:, :], c_dk[j * 128:(j + 1) * 128, :])
                kb.dma("sp", cst[1][:, :], c_dv[j * 128:(j + 1) * 128, :])
                kb.copy("pool", cb[:, :], cst[0][:, :])
                ksrc, vsrc = cb, cst[1]
                kcol = vcol = 0
            else:
                ksrc, vsrc, kcol, vcol = spb, spb, 512, 1024
            tb = kb.tbank[j % 2]
            for a in range(4):
                kb.tr(tb[:, a * nk:(a + 1) * nk], ksrc[0:nk, kcol + a * 128:kcol + (a + 1) * 128])
            kb.evac(KTd[:, :, j * 128:j * 128 + nk], tb[:, 0:4 * nk].rearrange("p (a t) -> p a t", a=4))
            kb.copy("dve", Vd[0:nk, j, :, :].rearrange("p h (a d) -> p h a d", a=2)[:, :, :, 0:64],
                    vsrc[0:nk, vcol:vcol + 512].rearrange("p (h a d) -> p h a d", h=4, a=2))
        for h in range(4):
            at.build_E(Esd[:, 0, h, :], g_scr[h, :], 480, 128, DSEQ)
            at.build_E(Esd[0:DSEQ, 1, h, :], g_scr[h, :], 608, DSEQ, DSEQ)
        for h in range(4):
            for m in range(2):
                at.run("softmax", 1, DSEQ,
                       lambda qb, h=h: [(kt, 128, None, True) for kt in range(15)] +
                                       [(15, 128, Esd[:, 0, h, :], False), (16, DSEQ, Esd[0:DSEQ, 1, h, :], False)],
                       lambda kt, h=h, m=m: KTd[64 * m:64 * m + 64, h, kt * 128:kt * 128 + (128 if kt < 16 else DSEQ)],
                       lambda qb, h=h, m=m: QTd[64 * m:64 * m + 64, h, :],
                       lambda kt, h=h: [Vd[0:(128 if kt < 16 else DSEQ), kt, h, 0:65],
                                        Vd[0:(128 if kt < 16 else DSEQ), kt, h, 65:130]],
                       sc_d, lambda qb, vi, on, h=h, m=m: kb.dma(
                           "pool", soT_d[m * 512 + h * 128 + vi * 64:m * 512 + h * 128 + (vi + 1) * 64, :], on),
                       far_bias=fb[:, h:h + 1])
        for hg in range(2):
            kb.op("pool", lambda e: e.memset(Vm[:, :, :, 64:65], 1.0), w=[BK])
            for j in range(17):
                nk = 128 if j < 16 else DSEQ
                if j < 16:
                    kb.dma("sp", cst[0][:, 0:128], c_ckv[j * 128:(j + 1) * 128, :])
                    kb.dma("sp", cst[1][:, 0:32], c_kr[j * 128:(j + 1) * 128, :])
                    kb.copy("pool", ckvb[:, :], cst[0][:, 0:128])
                    kb.copy("pool", krb[:, :], cst[1][:, 0:32])
                else:
                    kb.copy("pool", ckvb[0:nk, :], los[0:nk, 0:128])
                    kb.copy("pool", krb[0:nk, :], los[0:nk, 128:160])
                kb.transpose_cols(ckvT, ckvb, nk, 128, kb.tbank[1])
                kb.mm(kb.bank[0][0:nk, 0:256], ckvT[:, 0, 0:nk], wuk_b[:, 0, hg * 256:(hg + 1) * 256])
                kb.mm(kb.bank[1][0:nk, 0:256], ckvT[:, 0, 0:nk], wuv_b[:, 0, hg * 256:(hg + 1) * 256])
                kb.copy("act", kf[0:nk, 0:4, 0:64], kb.bank[0][0:nk, 0:256].rearrange("p (h d) -> p h d", h=4))
                for hh in range(4):
                    kb.copy("pool", kf[0:nk, hh, 64:96], krb[0:nk, :])
                kb.copy("dve", Vm[0:nk, j, :, 0:64], kb.bank[1][0:nk, 0:256].rearrange("p (h d) -> p h d", h=4))
                tb = kb.tbank[0]
                for hh in range(4):
                    kb.tr(tb[0:96, hh * nk:(hh + 1) * nk], kf[0:nk, hh, :])
                kb.evac(KTm[0:96, :, j * 128:j * 128 + nk], tb[0:96, 0:4 * nk].rearrange("p (a t) -> p a t", a=4))
            for hh in range(4):
                h = hg * 4 + hh
                at.run("softmax", 1, DSEQ,
                       lambda qb: [(kt, 128 if kt < 16 else DSEQ, None, False) for kt in range(17)],
                       lambda kt, hh=hh: KTm[0:96, hh, kt * 128:kt * 128 + (128 if kt < 16 else DSEQ)],
                       lambda qb, h=h: QTm[0:96, h, :],
                       lambda kt, hh=hh: [Vm[0:(128 if kt < 16 else DSEQ), kt, hh, 0:65]],
                       sc_m, lambda qb, vi, on, h=h: kb.dma("pool", soT_m[h * 64:(h + 1) * 64, :], on))
        kb.P.barrier()
        if STOP == 4:
            kb.P.finish()
            return nc

        kb.op("pool", lambda e: e.memset(BVv[:, :, 64:65], 1.0), w=[BV])
        kb.P.barrier()
        for t in range(NT):
            sl = t % 2
            r0 = t * 128
            latent_tile(xp[r0:r0 + 128, :], rope_p[r0:r0 + 128, :], 128, sl, o_lat[r0:r0 + 128, :], dbl=True)
            lo = lat_o[sl]
            for k in range(8):
                kb.mm(kb.bank[1][:, 0:256], hT_all[sl][:, k, :], wdp_b[:, k, :], start=(k == 0), stop=(k == 7))
            kb.copy("dve", dpsb[:, 0:256], kb.bank[1][:, 0:256])
            kb.dma("pool", o_dk[r0:r0 + 128, :], dpsb[:, 64:128])
            kb.dma("pool", o_dv[r0:r0 + 128, :], dpsb[:, 128:256])
            kb.copy("dve", dqkb[:, 0:128], dpsb[:, 0:128])
            kb.copy("act", dvt[sl][:, :].rearrange("p (a d) -> p a d", a=2)[:, :, 0:64],
                    dpsb[:, 128:256].rearrange("p (a d) -> p a d", a=2))
            kb.dma("pool", scr_dv[r0:r0 + 128, :], dvt[sl][:, :])
            for j in range(2):
                kb.mm(kb.bank[2][:, 0:96], cqT[:, j, :], wuqh_b[:, j, :], start=(j == 0), stop=(j == 1))
            kb.mm(kb.bank[2][:, 96:224], ckvT[:, 0, :], wukvh_b[:, 0, :])
            kb.copy("act", upsb[:, 0:224], kb.bank[2][:, 0:224])
            kb.copy("dve", qf[:, 0, 0:64], upsb[:, 0:64])
            rope(qf[:, 0, 64:96], upsb[:, 64:96], cs[sl], 128)
            kb.copy("act", kf[:, 0, 0:64], upsb[:, 96:160])
            kb.copy("dve", kf[:, 0, 64:96], lo[:, 128:160])
            kb.copy("dve", BVv[:, t, 0:64], upsb[:, 160:224], wk=["BV.%d" % t])
            tb = kb.tbank[1]
            kb.tr(tb[0:96, 0:128], qf[:, 0, :])
            kb.tr(tb[0:96, 128:256], kf[:, 0, :])
            kb.tr(tb[0:64, 256:384], dqkb[:, 0:64])
            kb.tr(tb[0:64, 384:512], dqkb[:, 64:128])
            kb.copy("act", BQ[0:96, r0:r0 + 128], tb[0:96, 0:128], wk=["BQ.%d" % t])
            kb.copy("dve", BK[0:96, r0:r0 + 128], tb[0:96, 128:256], wk=["BK.%d" % t])
            kb.copy("act", dT[sl][:, :, :], tb[0:64, 256:512].rearrange("p (a t) -> p a t", a=2))
            kb.dma("pool", scr_dqk[:, :, r0:r0 + 128], dT[sl][:, :, :])
        if STOP == 5:
            kb.P.finish()
            return nc
        at.run("softmax", SEQ // 512, 512,
               lambda qb: [(kt, 128, (Mcc[:, kt - 4 * qb, :] if kt >= 4 * qb else None), False)
                           for kt in range(4 * qb + 4)],
               lambda kt: BK[0:96, kt * 128:(kt + 1) * 128],
               lambda qb: BQ[0:96, qb * 512:(qb + 1) * 512],
               lambda kt: [BVv[:, kt, 0:65]],
               sc_m, lambda qb, vi, on: kb.dma("pool", oT_m[:, qb * 512:(qb + 1) * 512], on),
               kkeys=lambda kt: ["BK.%d" % kt], qkeys=lambda qb: ["BQ.%d" % (4 * qb + i) for i in range(4)],
               vkeys=lambda kt: ["BV.%d" % kt])
        kb.P.barrier()
        kb.dma("sp", BQ[0:64, :], scr_dqk[:, 0, :], wk=["BQc"])
        kb.dma("sp", BK[0:64, :], scr_dqk[:, 1, :], wk=["BKc"])
        for c in range(32):
            kb.dma("sp", BVv[:, c * 4:(c + 1) * 4, :],
                   scr_dv[c * 512:(c + 1) * 512, :].rearrange("(t p) d -> p t d", p=128), sem="BVc", wk=["BVc"])
        at.run("softmax", SEQ // 512, 512,
               lambda qb: [(kt, 128, (Ed[:, kt - 4 * qb + 1, :] if kt >= 4 * qb - 1 else None), kt < 4 * qb - 1)
                           for kt in range(4 * qb + 4)],
               lambda kt: BK[0:64, kt * 128:(kt + 1) * 128],
               lambda qb: BQ[0:64, qb * 512:(qb + 1) * 512],
               lambda kt: [BVv[:, kt, 0:65], BVv[:, kt, 65:130]],
               sc_d, lambda qb, vi, on: kb.dma("pool", oT_d[vi * 64:(vi + 1) * 64, qb * 512:(qb + 1) * 512], on),
               far_bias=fb[:, 4:5], kkeys=lambda kt: ["BKc"], qkeys=lambda qb: ["BQc"], vkeys=lambda kt: ["BVc"])
        kb.P.finish()
    return nc


_PROGS = {}
_CONSTS = {}


def _prog(name):
    if name not in _PROGS:
        _PROGS[name] = {"a0": build_attn0, "a1": build_attn1, "b0": lambda: build_dense(0),
                        "b1": lambda: build_dense(1)}[name]()
    return _PROGS[name]


def _const(name):
    if name not in _CONSTS:
        _CONSTS[name] = {"t5_oh": t5_onehot, "ca_oh": ca_onehot,
                         "rope_p": lambda: rope_table(np.arange(SEQ)),
                         "rope_s": lambda: rope_table(PAST + np.arange(DSEQ))}[name]()
    return _CONSTS[name]


def _run(name, in_maps):
    res = run_bass_kernel_spmd(_prog(name), in_maps, core_ids=list(range(NCORES)))
    return res.results


def _c(a):
    return np.ascontiguousarray(a, dtype=np.float32)


def kernel(x_prompt, x_sample, cache_mla_ckv, cache_mla_krope, cache_diff_k, cache_diff_v,
           cache_sb_k, cache_sb_v, cache_ca_k, cache_ca_v,
           norm_mix, norm_ff, w_in_even, mla_q_norm, mla_kv_norm, mla_w_uq, mla_w_ukv,
           diff_lambda_vecs, diff_subln, t5_bias, w_out_even, w_in_odd, ca_rel_bias, w_out_odd,
           w_ff1, w_ff2, final_norm):
    f = lambda a: np.asarray(a, dtype=np.float32)
    x_prompt, x_sample = f(x_prompt), f(x_sample)
    xp = _c(x_prompt[0])
    w_in0 = f(w_in_even)[0]
    w_uq = f(mla_w_uq)[0]
    wk = f(mla_w_ukv)[0].reshape(128, 8, 128)
    t5 = f(t5_bias)
    maps = []
    for c in range(NCORES):
        hd, m = c // 2, c % 2
        dpc = np.concatenate([np.arange(416 + hd * 128 + m * 64, 416 + hd * 128 + m * 64 + 64),
                              np.arange(928 + hd * 128 + m * 64, 928 + hd * 128 + m * 64 + 64),
                              np.arange(1440 + hd * 128, 1440 + hd * 128 + 128)])
        maps.append(dict(
            xp=xp, xs=_c(x_sample[c]), g_mix=_c(f(norm_mix)[0]), q_norm=_c(f(mla_q_norm)[0]),
            kv_norm=_c(f(mla_kv_norm)[0]), w_lat=_c(w_in0[:, 0:416]), w_dp=_c(w_in0[:, dpc]),
            w_dall=_c(w_in0[:, 416:]), w_uq=_c(w_uq), w_uk=_c(wk[:, :, 0:64].reshape(128, 512)),
            w_uv=_c(wk[:, :, 64:128].reshape(128, 512)), w_uqh=_c(w_uq[:, c * 96:(c + 1) * 96]),
            w_ukvh=_c(wk[:, c, :]), c_ckv=_c(f(cache_mla_ckv)[0, c]), c_kr=_c(f(cache_mla_krope)[0, c]),
            c_dk=_c(f(cache_diff_k)[0, c].reshape(PAST, 512)), c_dv=_c(f(cache_diff_v)[0, c].reshape(PAST, 512)),
            rope_p=_const("rope_p"), rope_s=_const("rope_s"), t5_oh=_const("t5_oh"),
            t5_tab=_c(np.concatenate([t5, t5[:, hd:hd + 1]], axis=1))))
    ra = _run("a0", maps)
    p_ckv = _c(ra[0]["o_lat"][:, 0:128]).reshape(1, 1, SEQ, 128)
    p_krope = _c(ra[0]["o_lat"][:, 128:160]).reshape(1, 1, SEQ, 32)
    p_dk = np.zeros((1, 1, SEQ, 4, 128), np.float32)
    p_dv = np.zeros((1, 1, SEQ, 4, 128), np.float32)
    s_ckv = np.zeros((1, 8, DSEQ, 128), np.float32)
    s_krope = np.zeros((1, 8, DSEQ, 32), np.float32)
    s_dk = np.zeros((1, 8, DSEQ, 4, 128), np.float32)
    s_dv = np.zeros((1, 8, DSEQ, 4, 128), np.float32)
    for c in range(NCORES):
        hd, m = c // 2, c % 2
        p_dk[0, 0, :, hd, m * 64:(m + 1) * 64] = ra[c]["o_dk"]
        if m == 0:
            p_dv[0, 0, :, hd, :] = ra[c]["o_dv"]
        s_ckv[0, c] = ra[c]["s_lat"][:, 0:128]
        s_krope[0, c] = ra[c]["s_lat"][:, 128:160]
        s_dk[0, c] = ra[c]["s_dk"].reshape(DSEQ, 4, 128)
        s_dv[0, c] = ra[c]["s_dv"].reshape(DSEQ, 4, 128)
    maps = []
    for r in range(NCORES):
        sl = slice(r * TOK, (r + 1) * TOK)
        mix = np.empty((1536, TB), np.float32)
        for h in range(8):
            mix[h * 64:(h + 1) * 64, :TOK] = ra[h]["oT_m"][:, sl]
        mix[0:512, TOK:] = ra[r]["soT_m"]
        for hd in range(4):
            for m in range(2):
                mix[512 + m * 512 + hd * 128:512 + m * 512 + (hd + 1) * 128, :TOK] = ra[2 * hd + m]["oT_d"][:, sl]
        mix[512:1536, TOK:] = ra[r]["soT_d"]
        maps.append(dict(x_in=_c(np.concatenate([x_prompt[0, sl], x_sample[r]], axis=0)), mixT=mix,
                         w_out=_c(f(w_out_even)[0]), w_ff1=_c(f(w_ff1)[0]), w_ff2=_c(f(w_ff2)[0]),
                         g_ff=_c(f(norm_ff)[0]), g_next=_c(f(norm_mix)[1]),
                         lamv=_c(f(diff_lambda_vecs)[0].reshape(256)), subln=_c(f(diff_subln)[0])))
    rb = _run("b0", maps)
    hp = _c(np.concatenate([rb[r]["hn_out"][:TOK] for r in range(NCORES)], axis=0))
    w_in1 = f(w_in_odd)[0]
    ca = f(ca_rel_bias)[0]
    maps = []
    for c in range(NCORES):
        cols = np.concatenate([np.arange(g * 512 + c * 64, g * 512 + (c + 1) * 64) for g in range(6)])
        maps.append(dict(hp=hp, hs=_c(rb[c]["hn_out"][TOK:]), w_p=_c(w_in1[:, cols]), w_s=_c(w_in1),
                         c_sbk=_c(f(cache_sb_k)[0, c].reshape(PAST, 512)), c_sbv=_c(f(cache_sb_v)[0, c].reshape(PAST, 512)),
                         c_cak=_c(f(cache_ca_k)[0, c].reshape(512, 512)), c_cav=_c(f(cache_ca_v)[0, c].reshape(512, 512)),
                         ca_oh=_const("ca_oh"), ca_tab=_c(np.concatenate([ca, ca[:, c:c + 1]], axis=1))))
    rc = _run("a1", maps)
    p_sbk = np.zeros((1, 1, SEQ, 8, 64), np.float32)
    p_sbv = np.zeros((1, 1, SEQ, 8, 64), np.float32)
    p_cak = np.zeros((1, 1, 512, 8, 64), np.float32)
    p_cav = np.zeros((1, 1, 512, 8, 64), np.float32)
    s_sbk = np.zeros((1, 8, DSEQ, 8, 64), np.float32)
    s_sbv = np.zeros((1, 8, DSEQ, 8, 64), np.float32)
    s_cak = np.zeros((1, 8, 512, 8, 64), np.float32)
    s_cav = np.zeros((1, 8, 512, 8, 64), np.float32)
    for c in range(NCORES):
        p_sbk[0, 0, :, c, :] = rc[c]["o_sk"]
        p_sbv[0, 0, :, c, :] = rc[c]["o_sv"]
        p_cak[0, 0, :, c, :] = rc[c]["o_ck"]
        p_cav[0, 0, :, c, :] = rc[c]["o_cv"]
        s_sbk[0, c] = rc[c]["s_sk"].reshape(DSEQ, 8, 64)
        s_sbv[0, c] = rc[c]["s_sv"].reshape(DSEQ, 8, 64)
        s_cak[0, c] = rc[c]["s_cak"].reshape(512, 8, 64)
        s_cav[0, c] = rc[c]["s_cav"].reshape(512, 8, 64)
    maps = []
    for r in range(NCORES):
        sl = slice(r * TOK, (r + 1) * TOK)
        mix = np.empty((1024, TB), np.float32)
        for h in range(8):
            mix[h * 64:(h + 1) * 64, :TOK] = rc[h]["oT_sb"][:, sl]
            mix[512 + h * 64:512 + (h + 1) * 64, :TOK] = rc[h]["oT_ca"][:, sl]
        mix[:, TOK:] = rc[r]["soT"]
        maps.append(dict(x_in=_c(rb[r]["x_out"]), mixT=mix, w_out=_c(f(w_out_odd)[0]), w_ff1=_c(f(w_ff1)[1]),
                         w_ff2=_c(f(w_ff2)[1]), g_ff=_c(f(norm_ff)[1]), g_next=_c(f(final_norm))))
    rd = _run("b1", maps)
    y_prompt = _c(np.concatenate([rd[r]["hn_out"][:TOK] for r in range(NCORES)], axis=0)).reshape(1, SEQ, D)
    y_sample = _c(np.stack([rd[r]["hn_out"][TOK:] for r in range(NCORES)], axis=0))
    return (y_prompt, y_sample, p_ckv, p_krope, p_dk, p_dv, p_sbk, p_sbv, p_cak, p_cav,
            s_ckv, s_krope, s_dk, s_dv, s_sbk, s_sbv, s_cak, s_cav)
```
